# Optimizing a Trainium2 kernel written in Bass

```python
import numpy as np
import jax
import jax.numpy as jnp
from jax import lax


D_MODEL = 1024
BATCH = 8
SEQ = 2048
DEPTH = 2

N_MIXERS = 4
GROUP_WIDTH = D_MODEL // N_MIXERS
D_MIX = N_MIXERS * GROUP_WIDTH
CONV_WIDTH = 31
POOL_WINDOWS = (2, 4, 8, 16)
POOL_GROUP = GROUP_WIDTH // len(POOL_WINDOWS)
SGU_HEADS = 4
SGU_HEAD_DIM = GROUP_WIDTH // SGU_HEADS
SGU_CHUNK = 128
ATT_HEADS = 4
HEAD_DIM = GROUP_WIDTH // ATT_HEADS
KV_DIM = HEAD_DIM
ROPE_DIM = HEAD_DIM // 4
ROPE_THETA = 500000.0
CMP_BLOCK = 32
CMP_STRIDE = 16
SLC_BLOCK = 64
N_SELECT = 8
N_LOCAL = 2
WINDOW = 512
Q_BLOCK = 128
N_BRANCH = 3
D_FF = 4 * D_MODEL
NORM_EPS = 1e-6
NEG_INF = -1e30
FORCE_SCORE = 1e9
IN_WIDTHS = (GROUP_WIDTH, GROUP_WIDTH, GROUP_WIDTH, GROUP_WIDTH, GROUP_WIDTH, ATT_HEADS * HEAD_DIM, KV_DIM, KV_DIM, KV_DIM, KV_DIM, KV_DIM, KV_DIM, ATT_HEADS * N_BRANCH)
D_IN = 5 * GROUP_WIDTH + ATT_HEADS * HEAD_DIM + 6 * KV_DIM + ATT_HEADS * N_BRANCH

kernel_name = 'hybrid_parallel_group_block'


def rms_norm(x, g):
    xf = x.astype(jnp.float32)
    y = xf * lax.rsqrt(jnp.mean(xf * xf, axis=-1, keepdims=True) + NORM_EPS)
    return (y * g.astype(jnp.float32)).astype(x.dtype)


def layer_norm(x, g, b):
    xf = x.astype(jnp.float32)
    mu = jnp.mean(xf, axis=-1, keepdims=True)
    var = jnp.mean(jnp.square(xf - mu), axis=-1, keepdims=True)
    y = (xf - mu) * lax.rsqrt(var + NORM_EPS)
    return (y * g.astype(jnp.float32) + b.astype(jnp.float32)).astype(x.dtype)


def rope(x, pos):
    half = ROPE_DIM // 2
    inv = ROPE_THETA ** (-jnp.arange(half, dtype=jnp.float32) * 2.0 / ROPE_DIM)
    ang = pos.astype(jnp.float32)[..., None] * inv
    cos = jnp.cos(ang)[:, :, None, :]
    sin = jnp.sin(ang)[:, :, None, :]
    xf = x.astype(jnp.float32)
    x1 = xf[..., :half]
    x2 = xf[..., half:ROPE_DIM]
    out = jnp.concatenate([x1 * cos - x2 * sin, x2 * cos + x1 * sin, xf[..., ROPE_DIM:]], axis=-1)
    return out.astype(x.dtype)


def conv_group(a, gate, w, b, ln_g, ln_b):
    h = a * jax.nn.sigmoid(gate)
    c = h.shape[-1]
    h = lax.conv_general_dilated(h, w[:, None, :], window_strides=(1,), padding=[(CONV_WIDTH - 1, 0)], dimension_numbers=('NWC', 'WIO', 'NWC'), feature_group_count=c) + b
    return jax.nn.silu(layer_norm(h, ln_g, ln_b))


def pool_group(p, w, scale):
    bsz, s, c = p.shape
    cs = jnp.pad(jnp.cumsum(p.astype(jnp.float32), axis=1), ((0, 0), (1, 0), (0, 0)))
    t = jnp.arange(s)
    outs = []
    for gi, win in enumerate(POOL_WINDOWS):
        sl = slice(gi * POOL_GROUP, (gi + 1) * POOL_GROUP)
        lo = jnp.maximum(t + 1 - win, 0)
        total = cs[:, 1:, sl] - cs[:, lo, sl]
        count = (t + 1 - lo).astype(jnp.float32)[None, :, None]
        mixed = (total / count - p[..., sl].astype(jnp.float32)).astype(p.dtype)
        outs.append(jnp.einsum('bsc,cd->bsd', mixed, w[gi]))
    return jnp.concatenate(outs, axis=-1) * scale


def sgu_group(u, v, ln_g, ln_b, w_s, b_s):
    bsz, s, c = u.shape
    u = jax.nn.gelu(u)
    v = layer_norm(jax.nn.gelu(v), ln_g, ln_b)
    nc = s // SGU_CHUNK
    v = v.reshape(bsz, nc, SGU_CHUNK, SGU_HEADS, SGU_HEAD_DIM)
    mask = jnp.tril(jnp.ones((SGU_CHUNK, SGU_CHUNK), dtype=bool))
    w = jnp.where(mask[None], w_s, 0)
    mixed = jnp.einsum('hts,bcshd->bcthd', w, v) + jnp.transpose(b_s)[None, None, :, :, None]
    return u * mixed.reshape(bsz, s, c)


def nsa_group(q, kc, vc, ks, vs, kw, vw, gates, positions, pe_k, w1_k, w2_k, pe_v, w1_v, w2_v):
    bsz, s, _ = q.shape
    scale = HEAD_DIM ** -0.5
    t = jnp.arange(s)
    q = rope(q.reshape(bsz, s, ATT_HEADS, HEAD_DIM), positions)

    n_cmp = (s - CMP_BLOCK) // CMP_STRIDE + 1
    blk_start = jnp.arange(n_cmp) * CMP_STRIDE
    blk_end = blk_start + CMP_BLOCK - 1
    blk_idx = blk_start[:, None] + jnp.arange(CMP_BLOCK)[None, :]

    def compress(xx, pe, w1, w2):
        xb = xx[:, blk_idx] + pe
        hid = jax.nn.gelu(jnp.einsum('bnf,fd->bnd', xb.reshape(bsz, n_cmp, CMP_BLOCK * KV_DIM), w1))
        return jnp.einsum('bnd,de->bne', hid, w2)

    k_cmp = compress(kc, pe_k, w1_k, w2_k)
    v_cmp = compress(vc, pe_v, w1_v, w2_v)
    k_cmp = rope(k_cmp[:, :, None], positions[:, blk_end])[:, :, 0]
    s_cmp = jnp.einsum('bthd,bnd->bhtn', q, k_cmp).astype(jnp.float32) * scale
    cmp_mask = blk_end[None, :] <= t[:, None]
    p_cmp = jax.nn.softmax(jnp.where(cmp_mask, s_cmp, NEG_INF), axis=-1) * cmp_mask
    o_cmp = jnp.einsum('bhtn,bnd->bthd', p_cmp.astype(v_cmp.dtype), v_cmp)

    n_slc = s // SLC_BLOCK
    k_sel = min(N_SELECT, n_slc)
    slc_start = jnp.arange(n_slc) * SLC_BLOCK
    overlap = jnp.clip(jnp.minimum(blk_start[:, None] + CMP_BLOCK, slc_start[None, :] + SLC_BLOCK) - jnp.maximum(blk_start[:, None], slc_start[None, :]), 0).astype(jnp.float32) / CMP_STRIDE
    imp = jnp.einsum('bhtn,nj->btj', p_cmp, overlap)
    q_blk = t // SLC_BLOCK
    j = jnp.arange(n_slc)
    back = q_blk[:, None] - j[None, :]
    forced = (j[None, :] == 0) | ((back >= 0) & (back < N_LOCAL))
    imp = jnp.where(forced, FORCE_SCORE, jnp.where(back < 0, -1.0, imp))
    _, sel = lax.top_k(imp, k_sel)

    k_blocks = rope(ks[:, :, None], positions)[:, :, 0].reshape(bsz, n_slc, SLC_BLOCK, KV_DIM)
    v_blocks = vs.reshape(bsz, n_slc, SLC_BLOCK, KV_DIM)
    nq = s // Q_BLOCK

    def slc_chunk(args):
        qc, sc, tc = args
        kg = jax.vmap(lambda kb, ix: kb[ix])(k_blocks, sc)
        vg = jax.vmap(lambda vb, ix: vb[ix])(v_blocks, sc)
        sco = jnp.einsum('bqhd,bqkld->bhqkl', qc, kg).astype(jnp.float32) * scale
        kpos = sc[..., None] * SLC_BLOCK + jnp.arange(SLC_BLOCK)
        mask = (kpos <= tc[None, :, None, None])[:, None]
        sco = jnp.where(mask, sco, NEG_INF).reshape(bsz, ATT_HEADS, Q_BLOCK, k_sel * SLC_BLOCK)
        p = jax.nn.softmax(sco, axis=-1).reshape(bsz, ATT_HEADS, Q_BLOCK, k_sel, SLC_BLOCK)
        return jnp.einsum('bhqkl,bqkld->bqhd', p.astype(vg.dtype), vg)

    q_chunks = jnp.transpose(q.reshape(bsz, nq, Q_BLOCK, ATT_HEADS, HEAD_DIM), (1, 0, 2, 3, 4))
    sel_chunks = jnp.transpose(sel.reshape(bsz, nq, Q_BLOCK, k_sel), (1, 0, 2, 3))
    o_slc = lax.map(slc_chunk, (q_chunks, sel_chunks, t.reshape(nq, Q_BLOCK)))
    o_slc = jnp.transpose(o_slc, (1, 0, 2, 3, 4)).reshape(bsz, s, ATT_HEADS, HEAD_DIM)

    kw_r = rope(kw[:, :, None], positions)[:, :, 0]
    pad = ((0, 0), (WINDOW, 0), (0, 0))
    kw_p = jnp.pad(kw_r, pad)
    vw_p = jnp.pad(vw, pad)
    span = WINDOW + Q_BLOCK
    band = jnp.arange(nq)[:, None] * Q_BLOCK + jnp.arange(span)[None, :]
    kb = kw_p[:, band]
    vb = vw_p[:, band]
    qb = q.reshape(bsz, nq, Q_BLOCK, ATT_HEADS, HEAD_DIM)
    sco = jnp.einsum('bcqhd,bckd->bhcqk', qb, kb).astype(jnp.float32) * scale
    kpos = band - WINDOW
    diff = t.reshape(nq, Q_BLOCK)[:, :, None] - kpos[:, None, :]
    wmask = (kpos[:, None, :] >= 0) & (diff >= 0) & (diff < WINDOW)
    p = jax.nn.softmax(jnp.where(wmask, sco, NEG_INF), axis=-1)
    o_win = jnp.einsum('bhcqk,bckd->bcqhd', p.astype(vb.dtype), vb).reshape(bsz, s, ATT_HEADS, HEAD_DIM)

    g = jax.nn.sigmoid(gates.reshape(bsz, s, ATT_HEADS, N_BRANCH))
    o = g[..., 0:1] * o_cmp + g[..., 1:2] * o_slc + g[..., 2:3] * o_win
    return o.reshape(bsz, s, ATT_HEADS * HEAD_DIM)


def setup_inputs(seed: int = 0) -> dict:
    key = jax.random.key(seed)
    keys = iter(jax.random.split(key, 32))
    f32 = jnp.float32
    L = DEPTH

    def nrm(shape, scale):
        return jax.random.normal(next(keys), shape, f32) * scale

    def gain(shape):
        return 1.0 + nrm(shape, 0.05)

    x = jax.random.normal(next(keys), (BATCH, SEQ, D_MODEL), f32)
    positions = jnp.arange(SEQ, dtype=jnp.int32)[None, :] + jax.random.randint(next(keys), (BATCH, 1), 0, 4096, dtype=jnp.int32)
    return {
        'x': x,
        'positions': positions,
        'pre_mix_norm': gain((L, D_MODEL)),
        'post_mix_norm': gain((L, D_MODEL)),
        'pre_ffn_norm': gain((L, D_MODEL)),
        'post_ffn_norm': gain((L, D_MODEL)),
        'w_in': nrm((L, D_MODEL, D_IN), D_MODEL ** -0.5),
        'conv_w': nrm((L, CONV_WIDTH, GROUP_WIDTH), CONV_WIDTH ** -0.5),
        'conv_b': nrm((L, GROUP_WIDTH), 0.02),
        'conv_ln_g': gain((L, GROUP_WIDTH)),
        'conv_ln_b': nrm((L, GROUP_WIDTH), 0.02),
        'pool_w': nrm((L, len(POOL_WINDOWS), POOL_GROUP, POOL_GROUP), POOL_GROUP ** -0.5),
        'pool_scale': 1.0 + nrm((L, GROUP_WIDTH), 0.1),
        'sgu_ln_g': gain((L, GROUP_WIDTH)),
        'sgu_ln_b': nrm((L, GROUP_WIDTH), 0.02),
        'sgu_w': nrm((L, SGU_HEADS, SGU_CHUNK, SGU_CHUNK), SGU_CHUNK ** -0.5),
        'sgu_b': 1.0 + nrm((L, SGU_HEADS, SGU_CHUNK), 0.05),
        'cmp_k_pe': nrm((L, CMP_BLOCK, KV_DIM), 0.1),
        'cmp_k_w1': nrm((L, CMP_BLOCK * KV_DIM, KV_DIM), (CMP_BLOCK * KV_DIM) ** -0.5),
        'cmp_k_w2': nrm((L, KV_DIM, KV_DIM), KV_DIM ** -0.5),
        'cmp_v_pe': nrm((L, CMP_BLOCK, KV_DIM), 0.1),
        'cmp_v_w1': nrm((L, CMP_BLOCK * KV_DIM, KV_DIM), (CMP_BLOCK * KV_DIM) ** -0.5),
        'cmp_v_w2': nrm((L, KV_DIM, KV_DIM), KV_DIM ** -0.5),
        'w_out': nrm((L, D_MIX, D_MODEL), D_MIX ** -0.5),
        'ffn_w1': nrm((L, D_MODEL, D_FF), D_MODEL ** -0.5),
        'ffn_w2': nrm((L, D_FF, D_MODEL), D_FF ** -0.5),
    }


def reference(x, positions, pre_mix_norm, post_mix_norm, pre_ffn_norm, post_ffn_norm, w_in, conv_w, conv_b, conv_ln_g, conv_ln_b, pool_w, pool_scale, sgu_ln_g, sgu_ln_b, sgu_w, sgu_b, cmp_k_pe, cmp_k_w1, cmp_k_w2, cmp_v_pe, cmp_v_w1, cmp_v_w2, w_out, ffn_w1, ffn_w2):
    splits = np.cumsum(IN_WIDTHS)[:-1].tolist()
    for l in range(DEPTH):
        h = rms_norm(x, pre_mix_norm[l])
        z = jnp.einsum('bsd,de->bse', h, w_in[l])
        (c_val, c_gate, p_in, s_u, s_v, q, kc, vc, ks, vs, kw, vw, g) = jnp.split(z, splits, axis=-1)
        y_conv = conv_group(c_val, c_gate, conv_w[l], conv_b[l], conv_ln_g[l], conv_ln_b[l])
        y_pool = pool_group(p_in, pool_w[l], pool_scale[l])
        y_sgu = sgu_group(s_u, s_v, sgu_ln_g[l], sgu_ln_b[l], sgu_w[l], sgu_b[l])
        y_nsa = nsa_group(q, kc, vc, ks, vs, kw, vw, g, positions, cmp_k_pe[l], cmp_k_w1[l], cmp_k_w2[l], cmp_v_pe[l], cmp_v_w1[l], cmp_v_w2[l])
        mix = jnp.concatenate([y_conv, y_pool, y_sgu, y_nsa], axis=-1)
        x = x + rms_norm(jnp.einsum('bse,ed->bsd', mix, w_out[l]), post_mix_norm[l])
        h = rms_norm(x, pre_ffn_norm[l])
        f = jnp.square(jax.nn.relu(jnp.einsum('bsd,df->bsf', h, ffn_w1[l])))
        x = x + rms_norm(jnp.einsum('bsf,fd->bsd', f, ffn_w2[l]), post_ffn_norm[l])
    return x
```

```python
import math
import numpy as np
from contextlib import ExitStack
import concourse.bass as bass
import concourse.mybir as mybir
from concourse.bass_utils import run_bass_kernel_spmd

F32 = mybir.dt.float32
BF16 = mybir.dt.bfloat16
I32 = mybir.dt.int32
AF = mybir.ActivationFunctionType
ALU = mybir.AluOpType
AX = mybir.AxisListType

S = 2048
D = 1024
NT = 16
DIN = 1932
DFF = 4096
L = 2
NCMP = 127
NEG = -30000.0
EPS = 1e-6
SCALE = 0.125
TWO_PI = 2.0 * math.pi


class Buf:
    __slots__ = ("name", "last_w", "reads", "excl")

    def __init__(self, name):
        self.name = name
        self.last_w = None
        self.reads = []
        self.excl = False


class Eng:
    def __init__(self, name, eng, sem, is_pe=False):
        self.name = name
        self.eng = eng
        self.sem = sem
        self.count = 0
        self.waited = {}
        self.is_pe = is_pe


class K:
    def __init__(self, nc, stack):
        self.nc = nc
        self.stack = stack
        mk = lambda n: stack.enter_context(nc.semaphore(n))
        self.pe = Eng("pe", nc.tensor, mk("s_pe"), is_pe=True)
        self.dve = Eng("dve", nc.vector, mk("s_dve"))
        self.act = Eng("act", nc.scalar, mk("s_act"))
        self.pool = Eng("pool", nc.gpsimd, mk("s_pool"))
        self.sp = Eng("sp", nc.sync, None)
        self.dma_sems = {}
        self.n_ops = 0
        self.n_waits = 0

    def buf(self, name):
        return Buf(name)

    def bufs(self, name, n):
        return [Buf(f"{name}{i}") for i in range(n)]

    def _wait(self, e, tok):
        sem, val = tok
        key = id(sem)
        if e.waited.get(key, 0) >= val:
            return
        e.eng.wait_ge(sem, val)
        e.waited[key] = val
        self.n_waits += 1

    def _deps(self, e, reads, writes):
        toks = []
        for b in reads:
            if b.last_w is not None:
                toks.append(b.last_w)
        for b in writes:
            if b.last_w is not None:
                toks.append(b.last_w)
            toks.extend(b.reads)
        for tok in toks:
            if e.is_pe and tok[0] is e.sem:
                continue
            self._wait(e, tok)

    def _commit(self, tok, reads, writes):
        for b in writes:
            b.last_w = tok
            b.reads = []
        for b in reads:
            if b not in writes:
                b.reads.append(tok)
                if len(b.reads) > 64:
                    b.reads = b.reads[-64:] if False else b.reads
        self.n_ops += 1

    def op(self, e, fn, reads=(), writes=()):
        reads = list(reads)
        writes = list(writes)
        for b in reads:
            if b.excl and b not in writes:
                writes.append(b)
        self._deps(e, reads, writes)
        inst = fn()
        e.count += 1
        inst.then_inc(e.sem, 1)
        tok = (e.sem, e.count)
        self._commit(tok, reads, writes)
        return tok

    def dma(self, out, in_, reads=(), writes=(), q=None, key=None, **kw):
        q = q or self.sp
        reads = list(reads)
        writes = list(writes)
        kb = key if key is not None else (writes[0] if writes else reads[0])
        if id(kb) not in self.dma_sems:
            s = self.stack.enter_context(self.nc.semaphore(f"d_{kb.name}"))
            self.dma_sems[id(kb)] = [s, 0]
        ent = self.dma_sems[id(kb)]
        self._deps(q, reads, writes)
        inst = q.eng.dma_start(out=out, in_=in_, **kw)
        ent[1] += 16
        inst.then_inc(ent[0], 16)
        tok = (ent[0], ent[1])
        self._commit(tok, reads, writes)
        return tok

    def finish(self, bufs, e=None):
        e = e or self.sp
        for b in bufs:
            if b.last_w is not None:
                self._wait(e, b.last_w)
            for t in b.reads:
                self._wait(e, t)


def _compact_reads(b):
    best = {}
    for sem, val in b.reads:
        kk = id(sem)
        if kk not in best or best[kk][1] < val:
            best[kk] = (sem, val)
    b.reads = list(best.values())


def host_consts():
    c = {}
    c["c_ident"] = np.eye(128, dtype=np.float32)
    t = np.arange(128)
    c["c_tril"] = (t[None, :] <= t[:, None]).astype(np.float32)
    c["c_triT"] = np.where(t[:, None] <= t[None, :], 0.0, NEG).astype(np.float32)
    c["c_tri2T"] = np.where(t[:, None] > t[None, :], 0.0, NEG).astype(np.float32)
    n = np.arange(NCMP)
    tt = np.arange(S)
    c["c_cmpbT"] = np.where(16 * n[:, None] + 31 <= tt[None, :], 0.0, NEG).astype(np.float32)
    j = np.arange(32)
    c["c_ebig"] = (j[:, None] == (tt[None, :] // 64)).astype(np.float32)
    ov = np.clip(np.minimum(16 * n[:, None] + 32, 64 * j[None, :] + 64) - np.maximum(16 * n[:, None], 64 * j[None, :]), 0, None) / 16.0
    c["c_overlap"] = ov.astype(np.float32)
    qb = tt // 64
    back = qb[:, None] - j[None, :]
    forced = (j[None, :] == 0) | ((back >= 0) & (back < 2))
    fut = back < 0
    A = np.where(forced | fut, 0.0, 1.0)
    Bm = np.where(forced, 1e9, np.where(fut, -1.0, 0.0))
    c["c_selA"] = A.astype(np.float32)
    c["c_selB"] = Bm.astype(np.float32)
    half = 8
    c["c_invf"] = (500000.0 ** (-np.arange(half, dtype=np.float32) * 2.0 / 16.0)).astype(np.float32)
    win = np.zeros((128, 2), np.float32)
    win[:64, 0] = 2; win[64:, 0] = 4; win[:64, 1] = 8; win[64:, 1] = 16
    c["c_invwin"] = (1.0 / win).astype(np.float32)
    tc = np.arange(16, dtype=np.float32)
    c["c_invcnt"] = (1.0 / np.minimum(tc[None, None, :] + 1.0, win[:, :, None])).astype(np.float32)
    return c


CONST_SHAPES = {
    "c_ident": [128, 128], "c_tril": [128, 128], "c_triT": [128, 128], "c_tri2T": [128, 128],
    "c_cmpbT": [NCMP, S], "c_ebig": [32, S], "c_overlap": [NCMP, 32], "c_selA": [S, 32], "c_selB": [S, 32],
    "c_invf": [8], "c_invwin": [128, 2], "c_invcnt": [128, 2, 16],
}

PARAM_SHAPES = {
    "pre_mix_norm": [L, D], "post_mix_norm": [L, D], "pre_ffn_norm": [L, D], "post_ffn_norm": [L, D],
    "w_in": [L, D, DIN], "conv_w": [L, 31, 256], "conv_b": [L, 256], "conv_ln_g": [L, 256], "conv_ln_b": [L, 256],
    "pool_w": [L, 4, 64, 64], "pool_scale": [L, 256], "sgu_ln_g": [L, 256], "sgu_ln_b": [L, 256],
    "sgu_w": [L, 4, 128, 128], "sgu_b": [L, 4, 128],
    "cmp_k_pe": [L, 32, 64], "cmp_k_w1": [L, 2048, 64], "cmp_k_w2": [L, 64, 64],
    "cmp_v_pe": [L, 32, 64], "cmp_v_w1": [L, 2048, 64], "cmp_v_w2": [L, 64, 64],
    "w_out": [L, D, D], "ffn_w1": [L, D, DFF], "ffn_w2": [L, DFF, D],
}


class _Stop(Exception):
    pass


def build_program(n_layers=L, taps=None, stop=None):
    taps = taps or []
    nc = bass.Bass("TRN2", target_bir_lowering=False)
    din = lambda n, s, dt=F32: nc.dram_tensor(n, list(s), dt, kind="ExternalInput").ap()
    x_d = din("x", [S, D])
    pos_d = din("positions", [S], I32)
    P = {n: din(n, s) for n, s in PARAM_SHAPES.items()}
    C = {n: din(n, s) for n, s in CONST_SHAPES.items()}
    out_d = nc.dram_tensor("out", [S, D], F32, kind="ExternalOutput").ap()
    tap_out = {}
    tap_keep = []

    with ExitStack() as st:
        k = K(nc, st)
        sb = lambda n, s, d=F32: st.enter_context(nc.sbuf_tensor(n, list(s), d))
        pe, dve, act, pool = k.pe, k.dve, k.act, k.pool
        V, A_, T, G = nc.vector, nc.scalar, nc.tensor, nc.gpsimd

        banks = [st.enter_context(nc.psum_tensor(f"ps{i}", [128, 512], F32)) for i in range(8)]
        BK = k.bufs("bank", 8)
        for b_ in BK:
            b_.excl = True
        ps_state = {"i": 0, "reserved": set()}

        def nextbank():
            while True:
                b = ps_state["i"] % 8
                ps_state["i"] += 1
                if b not in ps_state["reserved"]:
                    return b

        def bfv(b):
            return banks[b][:].bitcast(BF16)

        x_sb = sb("x_sb", [128, NT, D]); X = k.bufs("x", NT)
        RA = sb("RA", [128, 16384], BF16)
        RB = sb("RB", [128, 26688], BF16)
        RW = sb("RW", [128, 6144], BF16)
        identf = sb("identf", [128, 128]); identb = sb("identb", [128, 128], BF16)
        IDF = k.buf("idf"); IDB = k.buf("idb")
        triT = sb("triT", [128, 2, 128], BF16); TRI = k.buf("tri")
        ebig = sb("ebig", [128, S], BF16); EBIG = k.buf("ebig")
        selA = sb("selA", [128, NT, 32], BF16); selB = sb("selB", [128, NT, 32], BF16); SELAB = k.buf("selab")
        invf = sb("invf", [128, 8]); INVF = k.buf("invf")
        invwin = sb("invwin", [128, 2]); invcnt = sb("invcnt", [128, 2, 16]); INVW = k.buf("invw")
        cosT = sb("cosT", [128, NT, 8]); sinT = sb("sinT", [128, NT, 8]); CS = k.buf("cs")
        cosC = sb("cosC", [128, 8]); sinC = sb("sinC", [128, 8]); CSC = k.buf("csc")
        vaug = sb("vaug", [128, NT, 2, 65], BF16); VAUG = k.bufs("vaug", NT)
        vcmp = sb("vcmp", [128, 97], BF16); VCMP = k.buf("vcmp")
        onesb = sb("onesb", [128, 128], BF16); ONES = k.buf("ones")
        zerob = sb("zerob", [128, 272], BF16); ZERO = k.buf("zero")
        gsig = sb("gsig", [128, NT, 12]); GSIG = k.bufs("gsig", NT)
        gcol = sb("gcol", [128, 2, 8]); GCOL = k.buf("gcol")
        gpost = sb("gpost", [128, D]); GPOST = k.buf("gpost")
        convwT = sb("convwT", [128, 2, 31]); CONVW = k.buf("convw")
        cvcol = sb("cvcol", [128, 6]); CVCOL = k.buf("cvcol")
        bc5 = sb("bc5", [128, 5, 256]); BC5 = k.buf("bc5")
        poolw = sb("poolw", [128, 2, 128], BF16); POOLW = k.buf("poolw")
        poolsc = sb("poolsc", [128, 2]); POOLSC = k.buf("poolsc")
        sguwT = sb("sguwT", [128, 4, 128], BF16); SGUWT = k.buf("sguwT")
        sgub = sb("sgub", [1, 4, 128], BF16); SGUB = k.buf("sgub")
        w1kv = sb("w1kv", [128, 32, 64], BF16); W1KV = k.buf("w1kv")
        w2kv = sb("w2kv", [64, 2, 64], BF16); W2KV = k.buf("w2kv")
        peT = sb("peT", [128, 32], BF16); PET = k.buf("pet")
        cbias = sb("cbias", [64, 2]); CBIAS = k.buf("cbias")
        ss = sb("ss", [128, NT, 2]); SSQ = k.bufs("ss", NT)
        rstd = sb("rstd", [128, NT]); RSTD = k.bufs("rstd", NT)
        tmpA = sb("tmpA", [128, 4, 544]); TMPA = k.bufs("tmpA", 4)
        small = sb("small", [128, 4, 64]); SMALL = k.bufs("small", 4)
        posf = sb("posf", [128, NT]); POS = k.buf("pos")
        pocf = sb("pocf", [128, 1]); POSC = k.buf("posc")
        angfull = sb("angfull", [128, NT, 8]); ANGF = k.buf("angf")
        angc = sb("angc", [128, 8]); ANGC = k.buf("angc")
        epsb = sb("epsb", [128, 1]); EPSB = k.buf("epsb")
        imp = sb("imp", [128, 4, 32]); IMP = k.buf("imp")
        impm = sb("impm", [128, 4, 32]); IMPM = k.buf("impm")
        top8 = sb("top8", [128, 4, 8]); TOP8 = k.buf("top8")
        selb = sb("selb", [128, 4, 32], BF16); SELB = k.buf("selb")
        fin = sb("fin", [128, 4, 12]); FIN = k.bufs("fin", 4)

        ang = tmpA[:, 1, 0:128].rearrange("p (i j) -> p i j", i=NT)
        angk = tmpA[:, 2, 0:128].rearrange("p (i j) -> p i j", i=NT)
        angi = tmpA[:, 3, 0:128].bitcast(I32).rearrange("p (i j) -> p i j", i=NT)
        ANG = k.buf("ang")
        sguwf = tmpA[:, 0, 0:512].rearrange("p (h s) -> p h s", h=4)
        SGUWF = TMPA[0]
        hT = RA[:, 0:16384].rearrange("p (c t) -> p c t", c=8)
        HT = [[k.buf(f"hT{c}_{g}") for g in range(4)] for c in range(8)]
        HTALL = [HT[c][g] for c in range(8) for g in range(4)]
        convoT = RA[:, 0:4096].rearrange("p (c t) -> p c t", c=2)
        CONVO = k.bufs("convo", NT)
        diag = RA[:, 4096:4096 + 7936].rearrange("p (c k m) -> p c k m", c=2, k=31)
        DIAG = k.buf("diag")
        DIAGS = [k.bufs(f"diag{c}_", 31) for c in range(2)]
        DIAGALL = DIAGS[0] + DIAGS[1]
        chb = RA[:, 12032:14080].rearrange("p (a t) -> p a t", a=4)
        CHB = k.bufs("chb", 4)
        PTb = RA[:, 4096:6144].rearrange("p (n t) -> p n t", n=4)
        PTB = k.bufs("PT", 4)
        selbT = RA[:, 6144:8192]
        SELBT = k.bufs("selbT", 4)
        oacc = RA[:, 8192:10240].bitcast(F32).rearrange("p (i c) -> p i c", i=4)
        OACC = k.buf("oacc")
        OACCH = k.bufs("oacch", 4)
        oacc2 = RA[:, 13312:15360].bitcast(F32).rearrange("p (i c) -> p i c", i=4)
        OACCH2 = k.bufs("oacch2", 4)
        oaccs = [oacc, oacc2]; OACCHS = [OACCH, OACCH2]
        PTc = RA[:, 15648:16160]; PTC = k.buf("PTc")
        nsab = RA[:, 10240:11264].rearrange("p (i c) -> p i c", i=4)
        NSAB = k.buf("nsab")
        hidkv = RA[0:64, 11264:11520].rearrange("p (a n) -> p a n", a=2)
        HIDKV = k.buf("hidkv")
        kcmpT = RA[:, 15392:15648].rearrange("p (a n) -> p a n", a=2)
        KCMPT = k.buf("kcmpT")
        kcr = RA[:, 11648:11776]
        KCR = k.buf("kcr")
        ftmp = RA[:, 11776:11776 + 512].bitcast(F32).rearrange("p (i c) -> p i c", i=4)
        FTMP = k.buf("ftmp")
        kvcR = RA[:, 13312:13312 + 2080].rearrange("p (r m) -> p r m", r=16)
        KVCR = k.buf("kvcR")
        NSATMP = PTB + SELBT + OACCH + OACCH2 + [OACC, NSAB, HIDKV, KCMPT, KCR, FTMP, KVCR, PTC]
        woutA = RW[:, 0:6144].rearrange("p (c n) -> p c n", c=6)
        woutB = RA[:, 4096:6144].rearrange("p (c n) -> p c n", c=2)
        WOUT = k.buf("wout"); WOUTA = k.buf("woutA")
        def wout_c(c):
            return woutA[:, c, :] if c < 6 else woutB[:, c - 6, :]
        junkD = RA[:, 12288:13312]; JUNKD = k.buf("junkD")
        hT2 = RA[:, 0:4096].rearrange("p (c t) -> p c t", c=8)
        HT2 = k.bufs("hT2", 8)
        w2s = RA[:, 4096:10240].rearrange("p (b c n) -> p b c n", b=3, c=4)
        W2S = k.bufs("w2s", 3)
        hnF = RA[:, 10240:14336].rearrange("p (a n) -> p a n", a=4); HNF = k.bufs("hnF", 4)
        junkF = RA[:, 14336:15360]; JUNKF = k.buf("junkF")
        o = 0
        def carve(n_el):
            nonlocal o
            v = RB[:, o:o + n_el]
            o += n_el
            return v
        gT = carve(2 * 2080).rearrange("p (c t) -> p c t", c=2)
        mixedT = carve(2 * 2048).rearrange("p (c t) -> p c t", c=2)
        uT = carve(2 * 2048).rearrange("p (c t) -> p c t", c=2)
        vtok = carve(NT * 256).rearrange("p (i c) -> p i c", i=NT)
        kvcT = carve(2048)
        qT = carve(2 * 2048).rearrange("p (c t) -> p c t", c=2)
        kT_off = o
        kT = carve(2 * 2048).rearrange("p (c t) -> p c t", c=2)
        assert o == 26688, o
        hnM = RB[:, kT_off:kT_off + 3072].rearrange("p (a n) -> p a n", a=3); HNM = k.bufs("hnM", 3)
        junkM = RB[:, kT_off + 3072:kT_off + 4096]; JUNKM = k.buf("junkM")
        kThi = RB[:, 0:4096].rearrange("p (c t) -> p c t", c=2)
        KTHI = k.buf("kThi")
        GT = [[k.buf(f"gT{c}_{g}") for g in range(4)] for c in range(2)]
        GPAD = k.buf("gpad")
        MIXED = [[k.buf(f"mx{c}_{g}") for g in range(4)] for c in range(2)]
        UT = [[k.buf(f"uT{c}_{g}") for g in range(4)] for c in range(2)]
        VTOK = k.bufs("vtok", NT)
        KVC = k.bufs("kvc", 4)
        QT = k.bufs("qT", NT)
        KT = k.bufs("kT", NT)
        yD = [RB[:, 12352 + 2048 * n_:12352 + 2048 * (n_ + 1)].bitcast(F32) for n_ in range(3)] + [RB[:, kT_off:kT_off + 2048].bitcast(F32)]
        YD = k.bufs("yD", 4)
        RBMIX = [b_ for r_ in GT + MIXED + UT for b_ in r_] + [GPAD, KTHI] + VTOK + KVC + QT + KT + YD
        fT = RB[:, 0:16384].rearrange("p (c t) -> p c t", c=32)
        FT = k.bufs("fT", 32)
        ybuf = RB[:, 16384:24576].bitcast(F32).rearrange("p (i d) -> p i d", i=4)
        YB = k.bufs("yb", 4)
        winA = RW[:, 0:4096].rearrange("p (c n) -> p c n", c=8)
        WINA = k.buf("winA")
        winS = RW[:, 0:6144].rearrange("p (b c n) -> p b c n", b=6, c=8)
        WINS = k.bufs("winS", 6)
        w1s = RW[:, 0:6144].rearrange("p (b c n) -> p b c n", b=3, c=8)
        W1S = k.bufs("w1s", 3)

        def interleave(gens, width, extra=()):
            gens = list(gens)
            extra = list(extra)
            active = []
            while gens or active or extra:
                while gens and len(active) < width:
                    active.append(gens.pop(0))
                for g_ in list(active) + list(extra):
                    try:
                        next(g_)
                    except StopIteration:
                        (active if g_ in active else extra).remove(g_)

        def drain(gen):
            for _ in gen:
                pass

        def handoff(old, new):
            best = {}
            for b_ in old:
                toks = list(b_.reads)
                if b_.last_w is not None:
                    toks.append(b_.last_w)
                for sem, val in toks:
                    kk = id(sem)
                    if kk not in best or best[kk][1] < val:
                        best[kk] = (sem, val)
            for b_ in new:
                b_.last_w = None
                b_.reads = list(best.values())

        def tap(name, ap, reads):
            if name not in taps:
                return
            t = nc.dram_tensor("tap_" + name, list(ap.shape), ap.dtype, kind="ExternalOutput").ap()
            tap_out[name] = t
            tb_ = k.buf("tap_" + name)
            tap_keep.append(tb_)
            k.dma(t, ap, reads=list(reads), key=tb_)

        try:
            k.dma(identf[:], C["c_ident"], writes=[IDF])
            k.op(dve, lambda: V.tensor_copy(identb[:], identf[:]), reads=[IDF], writes=[IDB])
            xv = x_d.rearrange("(i p) d -> p i d", p=128)
            for i in range(0, NT, 2):
                k.dma(x_sb[:, i:i + 2, :], xv[:, i:i + 2, :], writes=X[i:i + 2])

            k.dma(triT[:, 0, :], C["c_triT"], writes=[TRI], q=pool)
            k.dma(triT[:, 1, :], C["c_tri2T"], writes=[TRI], q=pool)
            k.op(dve, lambda: V.memset(ebig[:], 0.0), writes=[EBIG])
            k.dma(ebig[0:32, :], C["c_ebig"], reads=[EBIG], writes=[EBIG], q=pool)
            k.dma(selA[:], C["c_selA"].rearrange("(i p) j -> p i j", p=128), writes=[SELAB], q=pool)
            k.dma(selB[:], C["c_selB"].rearrange("(i p) j -> p i j", p=128), writes=[SELAB], q=pool)
            k.dma(invf[:], C["c_invf"].partition_broadcast(128), writes=[INVF])
            k.dma(invwin[:], C["c_invwin"], writes=[INVW])
            k.dma(invcnt[:], C["c_invcnt"], writes=[INVW])
            k.op(dve, lambda: V.memset(onesb[:], 1.0), writes=[ONES])
            k.op(dve, lambda: V.memset(zerob[:], 0.0), writes=[ZERO])
            k.op(dve, lambda: V.memset(vaug[:, :, :, 64:65], 1.0), writes=VAUG)
            k.op(dve, lambda: V.memset(vcmp[:], 0.0), writes=[VCMP])
            k.op(dve, lambda: V.memset(vcmp[:, 64:65], 1.0), reads=[VCMP], writes=[VCMP])
            k.dma(vcmp[0:NCMP, 65:97], C["c_overlap"], writes=[VCMP], q=pool)
            k.op(dve, lambda: V.memset(poolw[:], 0.0), writes=[POOLW])
            k.op(dve, lambda: V.memset(epsb[:], EPS), writes=[EPSB])

            def setup_rope():
                def sincos(ang_ap, np_, three, cos_ap, sin_ap, RD, WR):
                    a2 = ang[0:np_] if three else ang[0:np_, 0, :]
                    k2 = angk[0:np_] if three else angk[0:np_, 0, :]
                    ki = angi[0:np_] if three else angi[0:np_, 0, :]
                    W_ = [ANG, TMPA[1], TMPA[2], TMPA[3]]
                    for which, dst in ((0, sin_ap), (1, cos_ap)):
                        sh = 0.0 if which == 0 else math.pi / 2
                        k.op(dve, lambda: V.tensor_scalar(k2, ang_ap, sh, 1.0 / TWO_PI, op0=ALU.add, op1=ALU.mult), reads=RD, writes=W_)
                        k.op(dve, lambda: V.tensor_copy(ki, k2), reads=W_, writes=W_)
                        k.op(dve, lambda: V.tensor_copy(k2, ki), reads=W_, writes=W_)
                        k.op(dve, lambda: V.scalar_tensor_tensor(a2, k2, -TWO_PI, ang_ap, op0=ALU.mult, op1=ALU.add), reads=RD + W_, writes=W_)
                        if sh:
                            k.op(dve, lambda: V.tensor_scalar(a2, a2, sh, None, op0=ALU.add), reads=W_, writes=W_)
                        k.op(dve, lambda: V.tensor_scalar(k2, a2, math.pi, -TWO_PI, op0=ALU.is_gt, op1=ALU.mult), reads=W_, writes=W_)
                        k.op(dve, lambda: V.tensor_tensor(a2, a2, k2, op=ALU.add), reads=W_, writes=W_)
                        k.op(dve, lambda: V.tensor_scalar(k2, a2, -math.pi, TWO_PI, op0=ALU.is_lt, op1=ALU.mult), reads=W_, writes=W_)
                        k.op(dve, lambda: V.tensor_tensor(a2, a2, k2, op=ALU.add), reads=W_, writes=W_)
                        k.op(dve, lambda: V.tensor_scalar(a2, a2, -3.1415925, 3.1415925, op0=ALU.max, op1=ALU.min), reads=W_, writes=W_)
                        k.op(act, lambda dst=dst: A_.activation(out=dst, in_=a2, func=AF.Sin), reads=W_, writes=WR)

                stg_i = tmpA[0:16, 0, 0:128].bitcast(I32)
                stg_f = tmpA[0:16, 0, 128:256]
                k.dma(stg_i, pos_d.rearrange("(i p) -> i p", p=128), writes=[TMPA[0]])
                k.op(dve, lambda: V.tensor_copy(stg_f, stg_i), reads=[TMPA[0]], writes=[TMPA[0]])
                b_ = nextbank()
                k.op(pe, lambda b_=b_: T.transpose(banks[b_][:, 0:16], stg_f, identf[0:16, 0:16]), reads=[TMPA[0], IDF], writes=[BK[b_]])
                k.op(dve, lambda b_=b_: V.tensor_copy(posf[:], banks[b_][:, 0:16]), reads=[BK[b_]], writes=[POS])
                k.op(dve, lambda: V.tensor_tensor(angfull[:], posf[:].unsqueeze(2).to_broadcast([128, NT, 8]),
                                                  invf[:].unsqueeze(1).to_broadcast([128, NT, 8]), op=ALU.mult),
                     reads=[POS, INVF], writes=[ANGF])
                sincos(angfull[:], 128, True, cosT[:], sinT[:], [ANGF], [CS])
                c_i = tmpA[0:NCMP, 0, 256:272].bitcast(I32)
                k.dma(c_i, pos_d[16:S].rearrange("(n r) -> n r", r=16), reads=[TMPA[0]], writes=[TMPA[0]])
                k.op(dve, lambda: V.memset(pocf[:], 0.0), writes=[POSC])
                k.op(dve, lambda: V.tensor_copy(pocf[0:NCMP, :], c_i[:, 15:16]), reads=[TMPA[0], POSC], writes=[POSC])
                k.op(dve, lambda: V.tensor_scalar(angc[:], invf[:], pocf[:, 0:1], None, op0=ALU.mult), reads=[POSC, INVF], writes=[ANGC])
                sincos(angc[:], 128, False, cosC[:], sinC[:], [ANGC], [CSC])
                tap("cos", cosT[:], [CS]); tap("sin", sinT[:], [CS]); tap("cosC", cosC[:], [CSC])

            def norm_stats(tiles, junkv, JUNKB):
                for i in tiles:
                    k.op(act, lambda i=i: A_.activation(out=junkv, in_=x_sb[:, i, :], func=AF.Square, accum_out=ss[:, i, 0:1]),
                         reads=[X[i]], writes=[JUNKB, SSQ[i]])
                    k.op(act, lambda i=i: A_.activation(out=ss[:, i, 1:2], in_=ss[:, i, 0:1], func=AF.Sqrt, scale=1.0 / D, bias=epsb[:, 0:1]),
                         reads=[SSQ[i], EPSB], writes=[SSQ[i]])
                    k.op(dve, lambda i=i: V.reciprocal(rstd[:, i:i + 1], ss[:, i, 1:2]), reads=[SSQ[i]], writes=[RSTD[i]])

            def rmsnorm_to_hT(which, tiles, dstT, DST, tcol0, hnv, HNB, junkv, JUNKB, stats=True):
                if stats:
                    norm_stats(tiles, junkv, JUNKB)
                if stop == "A0":
                    raise _Stop()
                hb = [nextbank() for _ in range(4)]
                for n_, i in enumerate(tiles):
                    hsel = n_ % len(HNB)
                    k.op(dve, lambda i=i, hsel=hsel: V.tensor_scalar(hnv[:, hsel, :], x_sb[:, i, :], rstd[:, i:i + 1], None, op0=ALU.mult),
                         reads=[X[i], RSTD[i]], writes=[HNB[hsel]])
                    def tr8(n_=n_, hsel=hsel):
                        r = None
                        for c in range(8):
                            dstp = bfv(hb[c // 2])[:, (c % 2) * 512 + n_ * 128:(c % 2) * 512 + (n_ + 1) * 128]
                            r = T.transpose(dstp, hnv[:, hsel, c * 128:(c + 1) * 128], identb[:])
                        return r
                    k.op(pe, tr8, reads=[HNB[hsel], IDB], writes=[BK[b_] for b_ in hb])
                if stop == "A1":
                    raise _Stop()
                for c in range(8):
                    b = hb[c // 2]
                    src = bfv(b)[:, (c % 2) * 512:(c % 2) * 512 + 512]
                    import os as _os
                    _ev = _os.environ.get("EV", "")
                    if _ev == "copy":
                        k.op(dve, lambda src=src, c=c: V.tensor_copy(dstT[:, c, tcol0:tcol0 + 512], src), reads=[BK[b], GCOL], writes=[DST[c]])
                    elif (c % 2 == 0 or _ev == "dve") and _ev != "act":
                        k.op(dve, lambda src=src, c=c: V.tensor_scalar(dstT[:, c, tcol0:tcol0 + 512], src, gcol[:, which, c:c + 1], None, op0=ALU.mult),
                             reads=[BK[b], GCOL], writes=[DST[c]])
                    else:
                        k.op(act, lambda src=src, c=c: A_.activation(out=dstT[:, c, tcol0:tcol0 + 512], in_=src, func=AF.Identity, scale=gcol[:, which, c:c + 1]),
                             reads=[BK[b], GCOL], writes=[DST[c]])

                if stop == "A2":
                    raise _Stop()

            def layernorm256(srcv, SRCB, ta, gi, bi, outv, OUTB):
                sm = small[:, ta, :]
                k.op(dve, lambda: V.bn_stats(sm[:, 0:6], srcv), reads=[SRCB], writes=[SMALL[ta]])
                yield
                k.op(dve, lambda: V.bn_aggr(sm[:, 8:10], sm[:, 0:6]), reads=[SMALL[ta]], writes=[SMALL[ta]])
                yield
                k.op(act, lambda: A_.activation(out=sm[:, 10:11], in_=sm[:, 9:10], func=AF.Sqrt, bias=epsb[:, 0:1]), reads=[SMALL[ta], EPSB], writes=[SMALL[ta]])
                yield
                k.op(dve, lambda: V.reciprocal(sm[:, 11:12], sm[:, 10:11]), reads=[SMALL[ta]], writes=[SMALL[ta]])
                yield
                k.op(dve, lambda: V.tensor_scalar(srcv, srcv, sm[:, 8:9], sm[:, 11:12], op0=ALU.subtract, op1=ALU.mult), reads=[SRCB, SMALL[ta]], writes=[SRCB])
                yield
                k.op(dve, lambda: V.tensor_tensor(srcv, srcv, bc5[:, gi, :], op=ALU.mult), reads=[SRCB, BC5], writes=[SRCB])
                yield
                k.op(dve, lambda: V.tensor_tensor(outv, srcv, bc5[:, bi, :], op=ALU.add), reads=[SRCB, BC5], writes=[OUTB] if OUTB is not SRCB else [SRCB])
                yield

            def rope(dst_lo, dst_hi, x1, x2, cb, sb_, t0, t1, RD, WR, TB):
                k.op(dve, lambda: V.tensor_tensor(t0, x1, cb, op=ALU.mult), reads=RD, writes=[TB])
                yield
                k.op(dve, lambda: V.tensor_tensor(t1, x2, sb_, op=ALU.mult), reads=RD + [TB], writes=[TB])
                yield
                k.op(dve, lambda: V.tensor_tensor(dst_lo, t0, t1, op=ALU.subtract), reads=[TB], writes=WR)
                yield
                k.op(dve, lambda: V.tensor_tensor(t0, x2, cb, op=ALU.mult), reads=RD + [TB], writes=[TB])
                yield
                k.op(dve, lambda: V.tensor_tensor(t1, x1, sb_, op=ALU.mult), reads=RD + [TB], writes=[TB])
                yield
                k.op(dve, lambda: V.tensor_tensor(dst_hi, t0, t1, op=ALU.add), reads=[TB], writes=WR)
                yield

            def load_col(dst, vec, B_):
                k.dma(dst, vec.rearrange("(c p) -> p c", p=128), writes=[B_], allow_slow_non_contiguous=True)

            first = True
            for l in range(n_layers):
                if not first:
                    handoff(HT2 + W2S + HNF + [JUNKD, JUNKF], HTALL)
                    handoff(FT + YB, RBMIX + HNM + [JUNKM])
                    handoff(W1S, [WINA] + WINS)
                else:
                    handoff([], HNM + [JUNKM])
                first = False
                stg = tmpA[:, 1, :]; stg2 = tmpA[:, 2, :]
                k.op(dve, lambda: V.memset(stg[0:32, 256:512], 0.0), writes=[TMPA[1]])
                k.dma(stg[0:8, 0:128], P["pre_mix_norm"][l].rearrange("(c p) -> c p", p=128), writes=[TMPA[1]])
                k.dma(stg[0:8, 128:256], P["pre_ffn_norm"][l].rearrange("(c p) -> c p", p=128), writes=[TMPA[1]])
                k.dma(stg[0:31, 256:512], P["conv_w"][l], writes=[TMPA[1]])
                k.dma(stg2[0:2, 0:128], P["pool_scale"][l].rearrange("(c p) -> c p", p=128), writes=[TMPA[2]])
                k.dma(stg2[0:32, 128:192], P["cmp_k_pe"][l], writes=[TMPA[2]])
                k.dma(stg2[0:32, 192:256], P["cmp_v_pe"][l], writes=[TMPA[2]])
                k.dma(stg2[0:2, 256:384], P["conv_b"][l].rearrange("(c p) -> c p", p=128), writes=[TMPA[2]])
                k.dma(stg2[2:4, 256:384], P["conv_ln_g"][l].rearrange("(c p) -> c p", p=128), writes=[TMPA[2]])
                k.dma(stg2[4:6, 256:384], P["conv_ln_b"][l].rearrange("(c p) -> c p", p=128), writes=[TMPA[2]])
                b_ = nextbank()
                for (src_, n_, c0_, TB_) in ((stg[0:8, 0:128], 8, 0, TMPA[1]), (stg[0:8, 128:256], 8, 8, TMPA[1]), (stg2[0:2, 0:128], 2, 16, TMPA[2]),
                                            (stg[0:32, 256:384], 32, 32, TMPA[1]), (stg[0:32, 384:512], 32, 64, TMPA[1]), (stg2[0:32, 128:256], 32, 96, TMPA[2]),
                                        (stg2[0:6, 256:384], 6, 128, TMPA[2])):
                    k.op(pe, lambda src_=src_, n_=n_, c0_=c0_, b_=b_: T.transpose(banks[b_][:, c0_:c0_ + n_], src_, identf[0:n_, 0:n_]), reads=[TB_, IDF], writes=[BK[b_]])
                k.op(dve, lambda b_=b_: V.tensor_copy(gcol[:].rearrange("p a c -> p (a c)"), banks[b_][:, 0:16]), reads=[BK[b_]], writes=[GCOL])
                k.op(dve, lambda b_=b_: V.tensor_copy(poolsc[:], banks[b_][:, 16:18]), reads=[BK[b_]], writes=[POOLSC])
                k.op(dve, lambda b_=b_: V.tensor_copy(convwT[:, 0, :], banks[b_][:, 32:63]), reads=[BK[b_]], writes=[CONVW])
                k.op(dve, lambda b_=b_: V.tensor_copy(convwT[:, 1, :], banks[b_][:, 64:95]), reads=[BK[b_]], writes=[CONVW])
                k.op(dve, lambda b_=b_: V.tensor_copy(peT[:], banks[b_][:, 96:128]), reads=[BK[b_]], writes=[PET])
                k.op(dve, lambda b_=b_: V.tensor_copy(cvcol[:], banks[b_][:, 128:134]), reads=[BK[b_]], writes=[CVCOL])
                for n_, nm in enumerate(["conv_b", "conv_ln_g", "conv_ln_b", "sgu_ln_g", "sgu_ln_b"]):
                    k.dma(bc5[:, n_, :], P[nm][l].partition_broadcast(128), writes=[BC5])
                for gi in range(4):
                    r0 = 64 * (gi % 2)
                    k.dma(poolw[r0:r0 + 64, gi // 2, r0:r0 + 64], P["pool_w"][l, gi], writes=[POOLW], q=pool)
                k.dma(sguwf, P["sgu_w"][l].rearrange("h t s -> t h s"), writes=[SGUWF])
                k.dma(sgub[:], P["sgu_b"][l:l + 1], writes=[SGUB], q=pool)
                k.dma(w1kv[0:64], P["cmp_k_w1"][l].rearrange("(l d) e -> d l e", d=64), writes=[W1KV], q=pool)
                k.dma(w1kv[64:128], P["cmp_v_w1"][l].rearrange("(l d) e -> d l e", d=64), writes=[W1KV], q=pool)
                k.dma(w2kv[:, 0, :], P["cmp_k_w2"][l], writes=[W2KV], q=pool)
                k.dma(w2kv[:, 1, :], P["cmp_v_w2"][l], writes=[W2KV], q=pool)
                k.op(pool, lambda: G.affine_select(out=sguwf, in_=sguwf, pattern=[[0, 4], [-1, 128]], compare_op=ALU.is_ge,
                                                   fill=0.0, base=0, channel_multiplier=1), reads=[SGUWF], writes=[SGUWF])
                b = nextbank()
                for h in range(4):
                    k.op(pe, lambda h=h, b=b: T.transpose(banks[b][:, h * 128:(h + 1) * 128], sguwf[:, h, :], identf[:]),
                         reads=[SGUWF, IDF], writes=[BK[b]])
                k.op(dve, lambda b=b: V.tensor_copy(sguwT[:].rearrange("p h t -> p (h t)"), banks[b][:, 0:512]), reads=[BK[b]], writes=[SGUWT])

                if stop == "S2":
                    break
                w_in_v = P["w_in"][l].rearrange("(c p) n -> p c n", p=128)
                for c in range(8):
                    k.dma(winA[:, c, :], w_in_v[:, c, 1024:1536], writes=[WINA], q=pool)
                for m_ in range(2):
                    k.dma(winS[:, 4 + m_], w_in_v[:, :, [256, 0][m_]:[256, 0][m_] + 128], writes=[WINS[4 + m_]], q=pool)

                norm_stats(list(range(NT)), junkM, JUNKM)
                for g in range(4):
                    rmsnorm_to_hT(0, list(range(4 * g, 4 * g + 4)), hT, [HT[c][g] for c in range(8)], g * 512, hnM, HNM, junkM, JUNKM, stats=False)
                if l == 0:
                    setup_rope()
                handoff(HNM + [JUNKM], KT)
                if l == 0:
                    tap("hT", hT, HTALL)
                if stop == "A":
                    break

                def b1_body(i):
                    g = i // 4
                    b = nextbank()
                    def mmg(b=b, i=i):
                        r = None
                        for c in range(8):
                            r = T.matmul(banks[b][:, 0:512], lhsT=hT[:, c, i * 128:(i + 1) * 128], rhs=winA[:, c, :], start=(c == 0), stop=(c == 7))
                        return r
                    k.op(pe, mmg, reads=[HT[c][g] for c in range(8)] + [WINA], writes=[BK[b]])
                    yield
                    ta = i % 4
                    sv = tmpA[:, ta, 0:256]
                    k.op(act, lambda b=b, sv=sv: A_.activation(out=sv, in_=banks[b][:, 0:256], func=AF.Gelu_apprx_tanh), reads=[BK[b]], writes=[TMPA[ta]])
                    yield from layernorm256(sv, TMPA[ta], ta, 3, 4, vtok[:, i, :], VTOK[i])
                    qr = tmpA[:, ta, 256:384].bitcast(BF16)
                    qps = banks[b][:, 256:512].rearrange("p (h d) -> p h d", h=4)
                    qrv = qr.rearrange("p (h d) -> p h d", h=4)
                    cb = cosT[:, i:i + 1, :].to_broadcast([128, 4, 8]); sb_ = sinT[:, i:i + 1, :].to_broadcast([128, 4, 8])
                    tq = tmpA[:, ta, 384:448].rearrange("p (a h d) -> p a h d", a=2, h=4)
                    k.op(act, lambda qr=qr, b=b: A_.copy(qr, banks[b][:, 256:512]), reads=[BK[b]], writes=[TMPA[ta]])
                    yield from rope(qrv[:, :, 0:8], qrv[:, :, 8:16], qps[:, :, 0:8], qps[:, :, 8:16], cb, sb_, tq[:, 0], tq[:, 1], [BK[b], CS], [TMPA[ta]], TMPA[ta])
                    b2 = nextbank()
                    k.op(pe, lambda b2=b2, qr=qr: (T.transpose(bfv(b2)[:, 0:128], qr[:, 0:128], identb[:]), T.transpose(bfv(b2)[:, 128:256], qr[:, 128:256], identb[:]))[1],
                         reads=[TMPA[ta], IDB], writes=[BK[b2]])
                    k.op(act, lambda b2=b2, i=i: A_.copy(qT[:, :, i * 128:(i + 1) * 128], bfv(b2)[:, 0:256].rearrange("p (c t) -> p c t", c=2)),
                         reads=[BK[b2]], writes=[QT[i]])
                interleave([b1_body(i) for i in range(NT)], 4)
                if l == 0:
                    tap("vtok", vtok, VTOK); tap("qT", qT, QT)
                if stop == "B1":
                    break

                for c in range(8):
                    k.dma(winA[:, c, 0:268], w_in_v[:, c, 1664:1932], writes=[WINA], q=pool)
                def b2_body(i):
                    g = i // 4
                    b = nextbank()
                    def mmg2(b=b, i=i):
                        r = None
                        for c in range(8):
                            r = T.matmul(banks[b][:, 0:268], lhsT=hT[:, c, i * 128:(i + 1) * 128], rhs=winA[:, c, 0:268], start=(c == 0), stop=(c == 7))
                        return r
                    k.op(pe, mmg2, reads=[HT[c][g] for c in range(8)] + [WINA], writes=[BK[b]])
                    yield
                    ta = i % 4
                    v4 = banks[b][:, 0:256].rearrange("p (h d) -> p h d", h=4)
                    k.op(act, lambda i=i, v4=v4: A_.copy(vaug[:, i, :, 0:64], v4[:, 1::2, :]), reads=[BK[b]], writes=[VAUG[i]])
                    k.op(act, lambda i=i, b=b: A_.activation(out=gsig[:, i, :], in_=banks[b][:, 256:268], func=AF.Sigmoid), reads=[BK[b]], writes=[GSIG[i]])
                    yield
                    kr = tmpA[:, ta, 0:128].bitcast(BF16).rearrange("p (a d) -> p a d", a=2)
                    tq = tmpA[:, ta, 384:416].rearrange("p (a h d) -> p a h d", a=2, h=2)
                    cb = cosT[:, i:i + 1, :].to_broadcast([128, 2, 8]); sb_ = sinT[:, i:i + 1, :].to_broadcast([128, 2, 8])
                    k.op(act, lambda kr=kr, v4=v4: A_.copy(kr[:, :, 0:64], v4[:, 0::2, :]), reads=[BK[b]], writes=[TMPA[ta]])
                    yield from rope(kr[:, :, 0:8], kr[:, :, 8:16], v4[:, 0::2, 0:8], v4[:, 0::2, 8:16], cb, sb_, tq[:, 0], tq[:, 1], [BK[b], CS], [TMPA[ta]], TMPA[ta])
                    k.op(dve, lambda kr=kr: V.tensor_copy(kr[:, :, 64:128], kr[:, :, 0:64]), reads=[TMPA[ta]], writes=[TMPA[ta]])
                    yield
                    b2 = nextbank()
                    k.op(pe, lambda b2=b2, kr=kr: (T.transpose(bfv(b2)[:, 0:128], kr[:, 0, :], identb[:]), T.transpose(bfv(b2)[:, 128:256], kr[:, 1, :], identb[:]))[1],
                         reads=[TMPA[ta], IDB], writes=[BK[b2]])
                    k.op(act, lambda b2=b2, i=i: A_.copy(kT[:, :, i * 128:(i + 1) * 128], bfv(b2)[:, 0:256].rearrange("p (c t) -> p c t", c=2)),
                         reads=[BK[b2]], writes=[KT[i]])
                interleave([b2_body(i) for i in range(NT)], 4)
                if l == 0:
                    tap("kT", kT, KT); tap("vaug", vaug[:], VAUG); tap("gsig", gsig[:], GSIG)
                if stop == "B2":
                    break

                B3_COLS = [256, 0, 384, 128, 512, 640, 768, 896, 1536]
                def ws_slot(n_):
                    return (4 + n_) % 6
                ws_loaded = {"n": 2}
                def load_upto(n_):
                    while ws_loaded["n"] <= min(n_, 8):
                        m_ = ws_loaded["n"]
                        k.dma(winS[:, ws_slot(m_)], w_in_v[:, :, B3_COLS[m_]:B3_COLS[m_] + 128], writes=[WINS[ws_slot(m_)]], q=pool)
                        ws_loaded["n"] += 1
                handoff([WINA], WINS[0:4])
                load_upto(5)

                def fm_mm(slot, g):
                    b = nextbank()
                    def f(b=b):
                        r = None
                        for c in range(8):
                            r = T.matmul(banks[b][:, 0:512], lhsT=winS[:, slot, c, :], rhs=hT[:, c, g * 512:(g + 1) * 512], start=(c == 0), stop=(c == 7))
                        return r
                    k.op(pe, f, reads=[HT[c][g] for c in range(8)] + [WINS[slot]], writes=[BK[b]])
                    return b

                k.op(dve, lambda: V.memset(gT[:, :, 0:32], 0.0), writes=[GPAD])
                sgT = RB[:, 16448:18496].bitcast(F32).rearrange("p (a n) -> p a n", a=2)
                SGT = k.bufs("sgT", 2)
                handoff(KVC, SGT)
                def convpair_stream():
                    for ch in range(2):
                        for g in range(4):
                            bg = fm_mm(ws_slot(2 * ch), g)
                            bv = fm_mm(ws_slot(2 * ch + 1), g)
                            ta = g % 2
                            sg = sgT[:, ta, :]
                            k.op(act, lambda sg=sg, bg=bg: A_.activation(out=sg, in_=banks[bg][:, 0:512], func=AF.Sigmoid), reads=[BK[bg]], writes=[SGT[ta]])
                            k.op(dve, lambda sg=sg, bv=bv, ch=ch, g=g: V.tensor_tensor(gT[:, ch, 32 + g * 512:32 + (g + 1) * 512], banks[bv][:, 0:512], sg, op=ALU.mult),
                                 reads=[BK[bv], SGT[ta]], writes=[GT[ch][g]])
                            yield
                def pool_stream():
                    for ch in range(2):
                        for g in range(4):
                            bp = fm_mm(ws_slot(4 + ch), g)
                            Pn = tmpA[:, g % 2, 0:528]; PB = TMPA[g % 2]
                            T1 = tmpA[:, 2, 0:528]; T2 = tmpA[:, 3, 0:528]
                            if g == 0:
                                k.op(dve, lambda Pn=Pn: V.memset(Pn[:, 0:16], 0.0), writes=[PB])
                            else:
                                Pp = tmpA[:, (g - 1) % 2, 0:528]
                                k.op(dve, lambda Pn=Pn, Pp=Pp: V.tensor_copy(Pn[:, 0:16], Pp[:, 512:528]), reads=[TMPA[(g - 1) % 2]], writes=[PB])
                            k.op(act, lambda Pn=Pn, bp=bp: A_.copy(Pn[:, 16:528], banks[bp][:, 0:512]), reads=[BK[bp]], writes=[PB])
                            yield
                            k.op(dve, lambda Pn=Pn: V.tensor_tensor(T1[:, 1:528], Pn[:, 1:528], Pn[:, 0:527], op=ALU.add), reads=[PB], writes=[TMPA[2]])
                            if ch == 0:
                                k.op(dve, lambda: V.tensor_tensor(T2[64:128, 3:528], T1[64:128, 3:528], T1[64:128, 1:526], op=ALU.add), reads=[TMPA[2]], writes=[TMPA[3]])
                            else:
                                k.op(dve, lambda: V.tensor_tensor(T2[:, 3:528], T1[:, 3:528], T1[:, 1:526], op=ALU.add), reads=[TMPA[2]], writes=[TMPA[3]])
                                k.op(dve, lambda: V.tensor_tensor(T1[:, 7:528], T2[:, 7:528], T2[:, 3:524], op=ALU.add), reads=[TMPA[3]], writes=[TMPA[2]])
                                k.op(dve, lambda: V.tensor_tensor(T2[64:128, 15:528], T1[64:128, 15:528], T1[64:128, 7:520], op=ALU.add), reads=[TMPA[2]], writes=[TMPA[3]])
                            for (r0, Tt, TB_) in ((0, T1, TMPA[2]), (64, T2, TMPA[3])):
                                k.op(dve, lambda r0=r0, Tt=Tt, Pn=Pn, ch=ch, g=g: V.scalar_tensor_tensor(
                                    mixedT[r0:r0 + 64, ch, g * 512:(g + 1) * 512], Tt[r0:r0 + 64, 16:528], invwin[r0:r0 + 64, ch:ch + 1], Pn[r0:r0 + 64, 16:528],
                                    op0=ALU.mult, op1=ALU.subtract), reads=[TB_, PB, INVW], writes=[MIXED[ch][g]])
                                if g == 0:
                                    sm = small[r0:r0 + 64, 0, 0:16]
                                    k.op(dve, lambda r0=r0, Tt=Tt, sm=sm, ch=ch: V.tensor_tensor(sm, Tt[r0:r0 + 64, 16:32], invcnt[r0:r0 + 64, ch, :], op=ALU.mult),
                                         reads=[TB_, INVW], writes=[SMALL[0]])
                                    k.op(dve, lambda r0=r0, sm=sm, Pn=Pn, ch=ch: V.tensor_tensor(mixedT[r0:r0 + 64, ch, 0:16], sm, Pn[r0:r0 + 64, 16:32], op=ALU.subtract),
                                         reads=[SMALL[0], PB], writes=[MIXED[ch][g]])
                def u_stream():
                    for ch in range(2):
                        for g in range(4):
                            bu = fm_mm(ws_slot(6 + ch), g)
                            k.op(act, lambda bu=bu, ch=ch, g=g: A_.activation(out=uT[:, ch, g * 512:(g + 1) * 512], in_=banks[bu][:, 0:512], func=AF.Gelu_apprx_tanh),
                                 reads=[BK[bu]], writes=[UT[ch][g]])
                            yield
                    for g in range(4):
                        bk_ = fm_mm(ws_slot(8), g)
                        k.op(dve, lambda bk_=bk_, g=g: V.tensor_copy(kvcT[:, g * 512:(g + 1) * 512], banks[bk_][:, 0:512]), reads=[BK[bk_]], writes=[KVC[g]])
                        yield
                interleave([pool_stream(), convpair_stream()], 2)
                handoff(SGT, KVC)
                load_upto(8)
                drain(u_stream())
                if l == 0:
                    tap("gT", gT, GT[0] + GT[1] + [GPAD]); tap("mixedT", mixedT, MIXED[0] + MIXED[1]); tap("uT", uT, UT[0] + UT[1]); tap("kvcT", kvcT, KVC)
                if stop == "B3":
                    break

                handoff(HTALL, CONVO + DIAGALL + CHB)
                handoff([WINA] + WINS, [WOUTA])
                w_out_v = P["w_out"][l].rearrange("(c p) n -> p c n", p=128)
                for c in range(6):
                    k.dma(woutA[:, c, :], w_out_v[:, c, :], writes=[WOUTA], q=pool)
                for ch in range(2):
                    for kk in range(31):
                        if kk % 2 == 0:
                            k.op(dve, lambda ch=ch, kk=kk: V.tensor_scalar(diag[:, ch, kk, :], identf[:], convwT[:, ch, kk:kk + 1], None, op0=ALU.mult),
                                 reads=[IDF, CONVW], writes=[DIAGS[ch][kk]])
                        else:
                            k.op(act, lambda ch=ch, kk=kk: A_.activation(out=diag[:, ch, kk, :], in_=identf[:], func=AF.Identity, scale=convwT[:, ch, kk:kk + 1]),
                                 reads=[IDF, CONVW], writes=[DIAGS[ch][kk]])
                def conv_stream():
                    for g in range(4):
                        hf = [tmpA[:, 0, 0:512], tmpA[:, 1, 0:512]]
                        mean = tmpA[:, 2, 0:512]; var = tmpA[:, 3, 0:512]
                        glist = sorted(set([max(0, (g * 512 - 30)) // 512, g]))
                        for ch in range(2):
                            b = nextbank()
                            def cmm(b=b, g=g, ch=ch):
                                r = None
                                for kk in range(31):
                                    r = T.matmul(banks[b][:, 0:512], lhsT=diag[:, ch, kk, :], rhs=gT[:, ch, 2 + kk + g * 512:2 + kk + g * 512 + 512],
                                                 start=(kk == 0), stop=(kk == 30))
                                return r
                            k.op(pe, cmm, reads=[GT[ch][g_] for g_ in glist] + [GPAD] + DIAGS[ch], writes=[BK[b]])
                            k.op(act, lambda b=b, ch=ch: A_.activation(out=hf[ch], in_=banks[b][:, 0:512], func=AF.Identity, bias=cvcol[:, ch:ch + 1]),
                                 reads=[BK[b], CVCOL], writes=[TMPA[ch]])
                            k.op(dve, lambda ch=ch: V.tensor_copy(chb[:, ch, :], hf[ch]), reads=[TMPA[ch]], writes=[CHB[ch]])
                            k.op(act, lambda ch=ch: A_.activation(out=chb[:, 2 + ch, :], in_=hf[ch], func=AF.Square), reads=[TMPA[ch]], writes=[CHB[2 + ch]])
                            yield
                        bs1 = nextbank(); bs2 = nextbank()
                        def smm1(bs1=bs1):
                            T.matmul(banks[bs1][:, 0:512], lhsT=onesb[:], rhs=chb[:, 0, :], start=True, stop=False)
                            return T.matmul(banks[bs1][:, 0:512], lhsT=onesb[:], rhs=chb[:, 1, :], start=False, stop=True)
                        def smm2_(bs2=bs2):
                            T.matmul(banks[bs2][:, 0:512], lhsT=onesb[:], rhs=chb[:, 2, :], start=True, stop=False)
                            return T.matmul(banks[bs2][:, 0:512], lhsT=onesb[:], rhs=chb[:, 3, :], start=False, stop=True)
                        k.op(pe, smm1, reads=[CHB[0], CHB[1], ONES], writes=[BK[bs1]])
                        k.op(pe, smm2_, reads=[CHB[2], CHB[3], ONES], writes=[BK[bs2]])
                        k.op(act, lambda bs1=bs1: A_.activation(out=mean, in_=banks[bs1][:, 0:512], func=AF.Identity, scale=1.0 / 256), reads=[BK[bs1]], writes=[TMPA[2]])
                        k.op(dve, lambda: V.tensor_tensor(var, mean, mean, op=ALU.mult), reads=[TMPA[2]], writes=[TMPA[3]])
                        k.op(dve, lambda bs2=bs2: V.scalar_tensor_tensor(var, banks[bs2][:, 0:512], 1.0 / 256, var, op0=ALU.mult, op1=ALU.subtract),
                             reads=[BK[bs2], TMPA[3]], writes=[TMPA[3]])
                        k.op(act, lambda: A_.activation(out=var, in_=var, func=AF.Sqrt, bias=epsb[:, 0:1]), reads=[TMPA[3], EPSB], writes=[TMPA[3]])
                        k.op(dve, lambda: V.reciprocal(var, var), reads=[TMPA[3]], writes=[TMPA[3]])
                        yield
                        for ch in range(2):
                            k.op(dve, lambda ch=ch: V.tensor_tensor(hf[ch], hf[ch], mean, op=ALU.subtract), reads=[TMPA[ch], TMPA[2]], writes=[TMPA[ch]])
                            k.op(dve, lambda ch=ch: V.tensor_tensor(hf[ch], hf[ch], var, op=ALU.mult), reads=[TMPA[ch], TMPA[3]], writes=[TMPA[ch]])
                            k.op(act, lambda ch=ch, g=g: A_.activation(out=convoT[:, ch, g * 512:(g + 1) * 512], in_=hf[ch], func=AF.Silu,
                                                                        scale=cvcol[:, 2 + ch:3 + ch], bias=cvcol[:, 4 + ch:5 + ch]),
                                 reads=[TMPA[ch], CVCOL], writes=CONVO[4 * g:4 * g + 4])
                            yield
                def plsgu_stream():
                    for ch in range(2):
                        for g in range(4):
                            b = nextbank()
                            k.op(pe, lambda b=b, ch=ch, g=g: T.matmul(banks[b][:, 0:512], lhsT=poolw[:, ch, :], rhs=mixedT[:, ch, g * 512:(g + 1) * 512], start=True, stop=True),
                                 reads=[POOLW, MIXED[ch][g]], writes=[BK[b]])
                            k.op(act, lambda b=b, ch=ch, g=g: A_.activation(out=mixedT[:, ch, g * 512:(g + 1) * 512], in_=banks[b][:, 0:512], func=AF.Identity, scale=poolsc[:, ch:ch + 1]),
                                 reads=[BK[b], POOLSC], writes=[MIXED[ch][g]])
                            yield
                    for pr in range(2):
                        for g in range(4):
                            b = nextbank()
                            def smm(b=b, pr=pr, g=g):
                                r = None
                                for ti in range(4):
                                    i = 4 * g + ti
                                    for hh in range(2):
                                        h = 2 * pr + hh
                                        o_ = banks[b][64 * hh:64 * hh + 64, ti * 128:(ti + 1) * 128]
                                        T.matmul(o_, lhsT=vtok[:, i, h * 64:(h + 1) * 64], rhs=sguwT[:, h, :], start=True, stop=False)
                                        r = T.matmul(o_, lhsT=onesb[0:1, 0:64], rhs=sgub[0:1, h, :], start=False, stop=True)
                                return r
                            k.op(pe, smm, reads=VTOK[4 * g:4 * g + 4] + [SGUWT, SGUB, ONES], writes=[BK[b]])
                            k.op(dve, lambda b=b, pr=pr, g=g: V.tensor_tensor(uT[:, pr, g * 512:(g + 1) * 512], banks[b][:, 0:512], uT[:, pr, g * 512:(g + 1) * 512], op=ALU.mult),
                                 reads=[BK[b], UT[pr][g]], writes=[UT[pr][g]])
                            yield
                interleave([conv_stream()], 1, extra=[plsgu_stream()])
                if l == 0:
                    tap("convoT", convoT, CONVO)
                if stop == "C1":
                    break
                handoff(DIAGALL + CHB, NSATMP)
                k.op(dve, lambda: V.memset(selbT[:, :], 0.0), writes=SELBT)
                handoff(GT[0] + GT[1] + [GPAD], [KTHI])
                k.op(dve, lambda: V.memset(kThi[0:64], 0.0), writes=[KTHI])
                k.op(act, lambda: A_.copy(kThi[64:128], kT[64:128]), reads=KT + [KTHI], writes=[KTHI])
                k.op(dve, lambda: V.memset(kT[64:128], 0.0), reads=[KTHI], writes=KT)

                if l == 0:
                    tap("poolT", mixedT, MIXED[0] + MIXED[1]); tap("sguT", uT, UT[0] + UT[1])
                if stop == "C3":
                    break

                cb_b = [nextbank(), nextbank()]
                for a in range(2):
                    def cbm(b=cb_b[a], a=a):
                        r = None
                        for l_ in range(32):
                            r = T.matmul(banks[b][0:64, 0:1], lhsT=w1kv[64 * a:64 * a + 64, l_, :], rhs=peT[64 * a:64 * a + 64, l_:l_ + 1], start=(l_ == 0), stop=(l_ == 31))
                        return r
                    k.op(pe, cbm, reads=[W1KV, PET], writes=[BK[cb_b[a]]])
                    k.op(dve, lambda a=a: V.tensor_copy(cbias[:, a:a + 1], banks[cb_b[a]][0:64, 0:1]), reads=[BK[cb_b[a]]], writes=[CBIAS])
                if stop == "C4a1":
                    break
                k.op(dve, lambda: V.memset(kvcR[:, :, 128:130], 0.0), writes=[KVCR])
                k.op(dve, lambda: V.tensor_copy(kvcR[:, :, 0:128], kvcT.rearrange("p (m r) -> p r m", r=16)), reads=KVC + [KVCR], writes=[KVCR])
                hb_ = [nextbank(), nextbank()]
                for a in range(2):
                    def hmm(b=hb_[a], a=a):
                        r = None
                        for l_ in range(32):
                            r = T.matmul(banks[b][0:64, 0:128], lhsT=w1kv[64 * a:64 * a + 64, l_, :],
                                         rhs=kvcR[64 * a:64 * a + 64, l_ % 16, (l_ // 16):(l_ // 16) + 128], start=(l_ == 0), stop=(l_ == 31))
                        return r
                    k.op(pe, hmm, reads=[W1KV, KVCR], writes=[BK[hb_[a]]])
                if stop == "C4a15":
                    break
                for a in range(2):
                    k.op(act, lambda a=a: A_.activation(out=hidkv[:, a, 0:NCMP], in_=banks[hb_[a]][0:64, 0:NCMP], func=AF.Gelu_apprx_tanh, bias=cbias[:, a:a + 1]),
                         reads=[BK[hb_[a]], CBIAS], writes=[HIDKV])
                if stop == "C4a2":
                    break
                b = nextbank()
                k.op(pe, lambda b=b: T.matmul(banks[b][0:NCMP, 0:64], lhsT=hidkv[:, 0, 0:NCMP], rhs=w2kv[:, 0, :], start=True, stop=True), reads=[HIDKV, W2KV], writes=[BK[b]])
                k.op(pe, lambda b=b: T.matmul(banks[b][0:NCMP, 64:128], lhsT=hidkv[:, 1, 0:NCMP], rhs=w2kv[:, 1, :], start=True, stop=True), reads=[HIDKV, W2KV], writes=[BK[b]])
                k.op(act, lambda b=b: A_.copy(vcmp[0:NCMP, 0:64], banks[b][0:NCMP, 64:128]), reads=[BK[b]], writes=[VCMP])
                if stop == "C4a3":
                    break
                k.op(dve, lambda: V.memset(kcr[:], 0.0), writes=[KCR])
                k.op(act, lambda b=b: A_.copy(kcr[0:NCMP, 0:64], banks[b][0:NCMP, 0:64]), reads=[BK[b], KCR], writes=[KCR])
                tq = ftmp[0:NCMP, 0, 0:16].rearrange("p (a d) -> p a d", a=2)
                drain(rope(kcr[0:NCMP, 0:8], kcr[0:NCMP, 8:16], banks[b][0:NCMP, 0:8], banks[b][0:NCMP, 8:16], cosC[0:NCMP, :], sinC[0:NCMP, :],
                           tq[:, 0], tq[:, 1], [BK[b], CSC], [KCR], FTMP))
                k.op(dve, lambda: V.tensor_copy(kcr[0:NCMP, 64:128], kcr[0:NCMP, 0:64]), reads=[KCR], writes=[KCR])
                if stop == "C4a4":
                    break
                b2 = nextbank()
                k.op(pe, lambda b2=b2: T.transpose(bfv(b2)[:, 0:NCMP], kcr[0:NCMP, :], identb[0:NCMP, 0:NCMP]), reads=[KCR, IDB], writes=[BK[b2]])
                k.op(dve, lambda: V.memset(kcmpT[:], 0.0), writes=[KCMPT])
                k.op(dve, lambda b2=b2: V.tensor_copy(kcmpT[0:64, 0, 0:NCMP], bfv(b2)[0:64, 0:NCMP]), reads=[BK[b2], KCMPT], writes=[KCMPT])
                k.op(dve, lambda b2=b2: V.tensor_copy(kcmpT[64:128, 1, 0:NCMP], bfv(b2)[64:128, 0:NCMP]), reads=[BK[b2], KCMPT], writes=[KCMPT])
                if l == 0:
                    tap("kcmpT", kcmpT[:, 0, :], [KCMPT]); tap("vcmp", vcmp[:], [VCMP])
                if stop == "C4a":
                    break

                handoff([KVCR], OACCH2)
                nsa_state = {"acc": 0, "pt": 0}

                def cmp_chain(h, g, own_pt):
                    i0 = 4 * g
                    gcols = slice(g * 512, (g + 1) * 512)
                    oacc = oaccs[g % 2]; OACCH = OACCHS[g % 2]
                    pr, hh = h // 2, h % 2
                    bS = nextbank()
                    k.op(pe, lambda: T.matmul(banks[bS][0:NCMP, 0:512], lhsT=kcmpT[:, hh, 0:NCMP], rhs=qT[:, pr, gcols], start=True, stop=True),
                         reads=[KCMPT] + QT[i0:i0 + 4], writes=[BK[bS]])
                    if own_pt:
                        ptv = PTc[0:NCMP, :]; PTBUF = PTC
                    else:
                        slot = nsa_state["pt"] % 4; nsa_state["pt"] += 1
                        ptv = PTb[0:NCMP, slot, :]; PTBUF = PTB[slot]
                    k.op(act, lambda: A_.activation(out=ptv, in_=banks[bS][0:NCMP, 0:512], func=AF.Exp, scale=SCALE),
                         reads=[BK[bS]], writes=[PTBUF])
                    k.op(pool, lambda: G.affine_select(out=ptv, in_=ptv, pattern=[[1, 512]], compare_op=ALU.is_ge,
                                                       fill=0.0, base=g * 512 - 31, channel_multiplier=-16), reads=[PTBUF], writes=[PTBUF])
                    yield
                    bO = nextbank()
                    def cpv():
                        r = None
                        for ti in range(4):
                            r = T.matmul(banks[bO][:, ti * 97:(ti + 1) * 97], lhsT=ptv[:, ti * 128:(ti + 1) * 128], rhs=vcmp[0:NCMP, :], start=True, stop=True)
                        return r
                    k.op(pe, cpv, reads=[PTBUF, VCMP], writes=[BK[bO]])
                    ov = banks[bO][:, 0:388].rearrange("p (t c) -> p t c", t=4)
                    fi = h % 4
                    rs = fin[:, fi, 0:4]; ri = fin[:, fi, 4:8]; gr = fin[:, fi, 8:12]
                    k.op(dve, lambda: V.tensor_scalar(rs, ov[:, :, 64], 1e-30, None, op0=ALU.max), reads=[BK[bO]], writes=[FIN[fi]])
                    k.op(dve, lambda: V.reciprocal(ri, rs), reads=[FIN[fi]], writes=[FIN[fi]])
                    k.op(dve, lambda: V.tensor_tensor(gr, gsig[:, i0:i0 + 4, 3 * h], ri, op=ALU.mult), reads=[FIN[fi]] + GSIG[i0:i0 + 4], writes=[FIN[fi]])
                    k.op(dve, lambda: V.tensor_tensor(oacc[:, :, h * 64:(h + 1) * 64], ov[:, :, 0:64], gr.unsqueeze(2).to_broadcast([128, 4, 64]), op=ALU.mult),
                         reads=[BK[bO], FIN[fi]], writes=[OACCH[h]])
                    if h == 0:
                        k.op(dve, lambda: V.tensor_tensor(imp[:], ov[:, :, 65:97], ri.unsqueeze(2).to_broadcast([128, 4, 32]), op=ALU.mult),
                             reads=[BK[bO], FIN[fi]], writes=[IMP])
                    else:
                        k.op(dve, lambda: V.tensor_tensor(impm[:], ov[:, :, 65:97], ri.unsqueeze(2).to_broadcast([128, 4, 32]), op=ALU.mult),
                             reads=[BK[bO], FIN[fi]], writes=[IMPM])
                        k.op(dve, lambda: V.tensor_tensor(imp[:], imp[:], impm[:], op=ALU.add), reads=[IMP, IMPM], writes=[IMP])
                    yield

                def selection(g):
                    i0 = 4 * g
                    gcols = slice(g * 512, (g + 1) * 512)
                    k.op(dve, lambda: V.tensor_tensor(impm[:], imp[:], selA[:, i0:i0 + 4, :], op=ALU.mult), reads=[IMP, SELAB], writes=[IMPM])
                    k.op(dve, lambda: V.tensor_tensor(impm[:], impm[:], selB[:, i0:i0 + 4, :], op=ALU.add), reads=[IMPM, SELAB], writes=[IMPM])
                    yield
                    for ti in range(4):
                        k.op(dve, lambda ti=ti: V.max(out=top8[:, ti, :], in_=impm[:, ti, :]), reads=[IMPM], writes=[TOP8])
                    yield
                    for ti in range(4):
                        k.op(dve, lambda ti=ti: V.tensor_scalar(selb[:, ti, :], impm[:, ti, :], top8[:, ti, 7:8], NEG, op0=ALU.is_lt, op1=ALU.mult),
                             reads=[IMPM, TOP8], writes=[SELB])
                    yield
                    b2 = nextbank()
                    def trs(b2=b2):
                        r = None
                        for ti in range(4):
                            r = T.transpose(bfv(b2)[0:32, ti * 128:(ti + 1) * 128], selb[:, ti, :], identb[:])
                        return r
                    k.op(pe, trs, reads=[SELB, IDB], writes=[BK[b2]])
                    k.op(dve, lambda b2=b2: V.tensor_copy(selbT[0:32, gcols], bfv(b2)[0:32, 0:512]), reads=[BK[b2]], writes=[SELBT[g]])
                    yield

                def pre_gen(g):
                    for h in range(4):
                        yield from cmp_chain(h, g, True)
                    yield from selection(g)

                def att_chain(br, h, g):
                    i0 = 4 * g
                    oacc = oaccs[g % 2]; OACCH = OACCHS[g % 2]
                    pr, hh = h // 2, h % 2
                    rows = slice(64 * hh, 64 * hh + 64)
                    bO = 4 + (nsa_state["acc"] % 4); nsa_state["acc"] += 1
                    k.op(pe, lambda: T.matmul(banks[bO][:, 0:260], lhsT=zerob[:, 0:128], rhs=zerob[:, 0:260], start=True, stop=False),
                         reads=[ZERO], writes=[BK[bO]])
                    yield
                    kt_lo = 0 if br == 0 else max(0, i0 - 4)
                    kk_ = kT if hh == 0 else kThi

                    def qk_list(bS, kt, qlo, qhi, ncol, col0):
                        mms = [(banks[bS][:, 0:ncol], kk_[:, br, kt * 128:(kt + 1) * 128], qT[:, pr, col0:col0 + ncol])]
                        lc = 0
                        if qlo == kt:
                            mms.append((banks[bS][:, 0:128], identb[:], triT[:, 0, :]))
                            lc = 128
                        if br == 0:
                            if ncol > lc:
                                mms.append((banks[bS][:, lc:ncol], ebig[:, kt * 128:(kt + 1) * 128], selbT[:, col0 + lc:col0 + ncol]))
                        else:
                            if qhi == kt + 4:
                                l2 = (qhi - qlo) * 128
                                mms.append((banks[bS][:, l2:l2 + 128], identb[:], triT[:, 1, :]))
                        return mms

                    def emit_qk(mms):
                        r = None
                        for n_, (o_, l_, r_) in enumerate(mms):
                            r = T.matmul(o_, lhsT=l_, rhs=r_, start=(n_ == 0), stop=(n_ == len(mms) - 1))
                        return r

                    def emit_pv(slot, kt, qlo, qhi):
                        r = None
                        for qt in range(qlo, qhi + 1):
                            lc = (qt - qlo) * 128
                            last = (kt == i0 + 3) and (qt == qhi)
                            r = T.matmul(banks[bO][:, (qt - i0) * 65:(qt - i0 + 1) * 65], lhsT=PTb[:, slot, lc:lc + 128], rhs=vaug[:, kt, br, :],
                                         start=False, stop=last)
                        return r

                    steps = []
                    for kt in range(kt_lo, i0 + 4):
                        qlo = max(kt, i0)
                        qhi = i0 + 3 if br == 0 else min(kt + 4, i0 + 3)
                        steps.append((kt, qlo, qhi, (qhi - qlo + 1) * 128, qlo * 128))
                    prev = None
                    for (kt, qlo, qhi, ncol, col0) in steps:
                        bS = nextbank()
                        mms = qk_list(bS, kt, qlo, qhi, ncol, col0)
                        rd = [KT[kt], KTHI, TRI, EBIG, IDB, ZERO, SELBT[g]] + QT[qlo:qhi + 1]
                        wr = [BK[bS]]
                        if prev is not None:
                            pslot, pkt, pqlo, pqhi = prev
                            rd += [PTB[pslot], VAUG[pkt]]
                            wr += [BK[bO]]
                            k.op(pe, lambda prev=prev, mms=mms: (emit_pv(*prev), emit_qk(mms))[1], reads=rd, writes=wr)
                        else:
                            k.op(pe, lambda mms=mms: emit_qk(mms), reads=rd, writes=wr)
                        slot = nsa_state["pt"] % 4; nsa_state["pt"] += 1
                        k.op(act, lambda bS=bS, slot=slot, ncol=ncol: A_.activation(out=PTb[:, slot, 0:ncol], in_=banks[bS][:, 0:ncol], func=AF.Exp, scale=SCALE),
                             reads=[BK[bS]], writes=[PTB[slot]])
                        yield
                        prev = (slot, kt, qlo, qhi)
                    k.op(pe, lambda prev=prev: emit_pv(*prev), reads=[PTB[prev[0]], VAUG[prev[1]]], writes=[BK[bO]])
                    yield
                    ov = banks[bO][:, 0:260].rearrange("p (t c) -> p t c", t=4)
                    fi = bO - 4
                    ri = fin[:, fi, 4:8]; gr = fin[:, fi, 8:12]
                    k.op(dve, lambda: V.reciprocal(ri, ov[:, :, 64]), reads=[BK[bO]], writes=[FIN[fi]])
                    k.op(dve, lambda: V.tensor_tensor(gr, gsig[:, i0:i0 + 4, 3 * h + 1 + br], ri, op=ALU.mult), reads=[FIN[fi]] + GSIG[i0:i0 + 4], writes=[FIN[fi]])
                    k.op(dve, lambda: V.tensor_tensor(ftmp[:], ov[:, :, 0:64], gr.unsqueeze(2).to_broadcast([128, 4, 64]), op=ALU.mult),
                         reads=[BK[bO], FIN[fi]], writes=[FTMP])
                    k.op(dve, lambda: V.tensor_tensor(oacc[:, :, h * 64:(h + 1) * 64], oacc[:, :, h * 64:(h + 1) * 64], ftmp[:], op=ALU.add),
                         reads=[FTMP, OACCH[h]], writes=[OACCH[h]])

                interleave([cmp_chain(h, 0, False) for h in range(4)], 4)
                drain(selection(0))
                ps_state["reserved"] = {4, 5, 6, 7}
                for g in range(4):
                    i0 = 4 * g
                    gcols = slice(g * 512, (g + 1) * 512)
                    oacc = oaccs[g % 2]; OACCH = OACCHS[g % 2]
                    gens = [att_chain(br, h, g) for br in range(2) for h in range(4)]
                    interleave(gens, 4, extra=[pre_gen(g + 1)] if g + 1 < 4 else [])
                    k.op(act, lambda oacc=oacc: A_.copy(nsab[:], oacc[:]), reads=OACCH, writes=[NSAB])
                    b2 = nextbank()
                    def tro(b2=b2):
                        r = None
                        for ti in range(4):
                            for c in range(2):
                                r = T.transpose(bfv(b2)[:, c * 512 + ti * 128:c * 512 + (ti + 1) * 128], nsab[:, ti, c * 128:(c + 1) * 128], identb[:])
                        return r
                    k.op(pe, tro, reads=[NSAB, IDB], writes=[BK[b2]])
                    k.op(dve, lambda b2=b2, gcols=gcols: V.tensor_copy(qT[:, :, gcols], bfv(b2)[:, 0:1024].rearrange("p (c t) -> p c t", c=2)), reads=[BK[b2]], writes=QT[i0:i0 + 4])
                ps_state["reserved"] = set()
                if l == 0:
                    tap("nsaT", qT, QT)
                if stop == "C4":
                    break

                handoff(NSATMP, [WOUT, JUNKD])
                handoff(VTOK + KVC + KT, YD)
                w_out_v = P["w_out"][l].rearrange("(c p) n -> p c n", p=128)
                for c in range(8):
                    if c >= 6:
                        k.dma(wout_c(c), w_out_v[:, c, :], writes=[WOUT], q=pool)
                k.dma(gpost[:], P["post_mix_norm"][l].partition_broadcast(128), writes=[GPOST])
                mixsrc = [(convoT, 0, lambda i: [CONVO[i]]), (convoT, 1, lambda i: [CONVO[i]]),
                          (mixedT, 0, lambda i: [MIXED[0][i // 4]]), (mixedT, 1, lambda i: [MIXED[1][i // 4]]),
                          (uT, 0, lambda i: [UT[0][i // 4]]), (uT, 1, lambda i: [UT[1][i // 4]]),
                          (qT, 0, lambda i: [QT[i]]), (qT, 1, lambda i: [QT[i]])]

                def post_norm_residual(i, yv, YBUFS, junkD=junkD, JUNKD=JUNKD):
                    k.op(act, lambda: A_.activation(out=junkD, in_=yv, func=AF.Square, accum_out=ss[:, i, 0:1]), reads=YBUFS, writes=[JUNKD, SSQ[i]])
                    yield
                    k.op(act, lambda: A_.activation(out=ss[:, i, 1:2], in_=ss[:, i, 0:1], func=AF.Sqrt, scale=1.0 / D, bias=epsb[:, 0:1]), reads=[SSQ[i], EPSB], writes=[SSQ[i]])
                    yield
                    k.op(dve, lambda: V.reciprocal(rstd[:, i:i + 1], ss[:, i, 1:2]), reads=[SSQ[i]], writes=[RSTD[i]])
                    yield
                    k.op(dve, lambda: V.tensor_tensor(yv, yv, gpost[:], op=ALU.mult), reads=YBUFS + [GPOST], writes=YBUFS)
                    yield
                    k.op(dve, lambda: V.scalar_tensor_tensor(x_sb[:, i, :], yv, rstd[:, i:i + 1], x_sb[:, i, :], op0=ALU.mult, op1=ALU.add),
                         reads=YBUFS + [RSTD[i], X[i]], writes=[X[i]])
                    yield

                def d_body(i):
                    ysel = i % 4
                    yv = yD[ysel]
                    YBUFS = [YD[ysel]]
                    for dh in range(2):
                        b = nextbank()
                        def omm(b=b, i=i, dh=dh):
                            r = None
                            for c in range(8):
                                src, cc, _ = mixsrc[c]
                                r = T.matmul(banks[b][:, 0:512], lhsT=src[:, cc, i * 128:(i + 1) * 128], rhs=wout_c(c)[:, dh * 512:(dh + 1) * 512], start=(c == 0), stop=(c == 7))
                            return r
                        rd = [WOUT, WOUTA]
                        for c in range(8):
                            rd += mixsrc[c][2](i)
                        k.op(pe, omm, reads=rd, writes=[BK[b]])
                        yield
                        k.op(act, lambda b=b, yv=yv, dh=dh: A_.copy(yv[:, dh * 512:(dh + 1) * 512], banks[b][:, 0:512]), reads=[BK[b]], writes=YBUFS)
                        yield
                    yield from post_norm_residual(i, yv, YBUFS)
                interleave([d_body(i) for i in range(NT)], 4)
                if l == 0:
                    tap("x1", x_sb[:], X)
                if stop == "D":
                    break

                handoff(CONVO + [WOUT, JUNKD], HT2 + W2S + HNF + [JUNKF])
                handoff(RBMIX, FT + YB)
                handoff([WOUTA], W1S)
                k.dma(gpost[:], P["post_ffn_norm"][l].partition_broadcast(128), writes=[GPOST])
                w1_v = P["ffn_w1"][l].rearrange("(c p) n -> p c n", p=128)
                w2_v = P["ffn_w2"][l].rearrange("(s c p) d -> p s c d", c=4, p=128)
                ps_state["reserved"] = {4, 5, 6, 7}
                def ffn_norm1(tiles):
                    for n_, i in enumerate(tiles):
                        k.op(act, lambda i=i: A_.activation(out=junkF, in_=x_sb[:, i, :], func=AF.Square, accum_out=ss[:, i, 0:1]),
                             reads=[X[i]], writes=[JUNKF, SSQ[i]])
                        k.op(act, lambda i=i: A_.activation(out=ss[:, i, 1:2], in_=ss[:, i, 0:1], func=AF.Sqrt, scale=1.0 / D, bias=epsb[:, 0:1]),
                             reads=[SSQ[i], EPSB], writes=[SSQ[i]])
                        k.op(dve, lambda i=i: V.reciprocal(rstd[:, i:i + 1], ss[:, i, 1:2]), reads=[SSQ[i]], writes=[RSTD[i]])
                        k.op(dve, lambda i=i, n_=n_: V.tensor_scalar(hnF[:, n_, :], x_sb[:, i, :], rstd[:, i:i + 1], None, op0=ALU.mult),
                             reads=[X[i], RSTD[i]], writes=[HNF[n_]])

                def ffn_norm2():
                    hb = [nextbank() for _ in range(4)]
                    for n_ in range(4):
                        def tr8f(n_=n_):
                            r = None
                            for c in range(8):
                                dstp = bfv(hb[c // 2])[:, (c % 2) * 512 + n_ * 128:(c % 2) * 512 + (n_ + 1) * 128]
                                r = T.transpose(dstp, hnF[:, n_, c * 128:(c + 1) * 128], identb[:])
                            return r
                        k.op(pe, tr8f, reads=[HNF[n_], IDB], writes=[BK[b_] for b_ in hb])
                    for c in range(8):
                        b = hb[c // 2]
                        src = bfv(b)[:, (c % 2) * 512:(c % 2) * 512 + 512]
                        if c % 2 == 0:
                            k.op(dve, lambda src=src, c=c: V.tensor_scalar(hT2[:, c, :], src, gcol[:, 1, c:c + 1], None, op0=ALU.mult),
                                 reads=[BK[b], GCOL], writes=[HT2[c]])
                        else:
                            k.op(act, lambda src=src, c=c: A_.activation(out=hT2[:, c, :], in_=src, func=AF.Identity, scale=gcol[:, 1, c:c + 1]),
                                 reads=[BK[b], GCOL], writes=[HT2[c]])

                NW1, NW2 = 3, 3
                ffn_norm1(list(range(0, 4)))
                ffn_norm2()
                for g in range(4):
                    i0 = 4 * g
                    n1 = 16
                    def load_w2(j):
                        dh_, s8 = j // 8, j % 8
                        k.dma(w2s[:, j % NW2], w2_v[:, s8, :, dh_ * 512:(dh_ + 1) * 512], writes=[W2S[j % NW2]], q=pool)
                    def load_w1(s_):
                        k.dma(w1s[:, s_ % NW1], w1_v[:, :, s_ * 256:(s_ + 1) * 256], writes=[W1S[s_ % NW1]], q=pool)
                    if g == 0:
                        for s_ in range(NW1 - 1):
                            load_w1(s_)
                    for j in range(NW2 - 1):
                        load_w2(j)
                    for s_ in range(n1):
                        if s_ + NW1 - 1 < n1:
                            load_w1(s_ + NW1 - 1)
                        bb = [nextbank(), nextbank()]
                        def fmm(bb=bb, s_=s_):
                            r = None
                            for fc in range(2):
                                for c in range(8):
                                    r = T.matmul(banks[bb[fc]][:, 0:512], lhsT=w1s[:, s_ % NW1, c, fc * 128:(fc + 1) * 128], rhs=hT2[:, c, :], start=(c == 0), stop=(c == 7))
                            return r
                        k.op(pe, fmm, reads=HT2 + [W1S[s_ % NW1]], writes=[BK[bb[0]], BK[bb[1]]])
                        for fc in range(2):
                            b = bb[fc]
                            ta = (2 * s_ + fc) % 4
                            rl = tmpA[:, ta, 0:512]
                            k.op(act, lambda b=b, rl=rl: A_.activation(out=rl, in_=banks[b][:, 0:512], func=AF.Relu), reads=[BK[b]], writes=[TMPA[ta]])
                            k.op(dve, lambda rl=rl, s_=s_, fc=fc: V.tensor_tensor(fT[:, 2 * s_ + fc, :], rl, rl, op=ALU.mult), reads=[TMPA[ta]], writes=[FT[2 * s_ + fc]])
                    if g + 1 < 4:
                        ffn_norm1(list(range(i0 + 4, i0 + 8)))
                    for j in range(16):
                        dh, s8 = j // 8, j % 8
                        if j + NW2 - 1 < 16:
                            load_w2(j + NW2 - 1)
                        def wmm(j=j, s8=s8):
                            r = None
                            for ti in range(4):
                                for c in range(4):
                                    r = T.matmul(banks[4 + ti][:, 0:512], lhsT=fT[:, s8 * 4 + c, ti * 128:(ti + 1) * 128], rhs=w2s[:, j % NW2, c, :],
                                                 start=(s8 == 0 and c == 0), stop=(s8 == 7 and c == 3))
                            return r
                        k.op(pe, wmm, reads=FT[s8 * 4:s8 * 4 + 4] + [W2S[j % NW2]], writes=[BK[4], BK[5], BK[6], BK[7]])
                        if s8 == 7:
                            for ti in range(4):
                                k.op(act, lambda ti=ti, dh=dh: A_.copy(ybuf[:, ti, dh * 512:(dh + 1) * 512], banks[4 + ti][:, 0:512]), reads=[BK[4 + ti]], writes=[YB[ti]])
                        if j == 12 and g + 1 < 4:
                            ffn_norm2()
                        if j == 4 and g + 1 < 4:
                            for s_ in range(NW1 - 1):
                                load_w1(s_)
                    interleave([post_norm_residual(i0 + ti, ybuf[:, ti, :], [YB[ti]], junkF, JUNKF) for ti in range(4)], 4)
                ps_state["reserved"] = set()
                if l == 0:
                    tap("x2", x_sb[:], X)

        except _Stop:
            pass
        ov_ = out_d.rearrange("(i p) d -> p i d", p=128)
        OUTB = k.bufs("outd", NT // 2)
        for i in range(0, NT, 2):
            k.dma(ov_[:, i:i + 2, :], x_sb[:, i:i + 2, :], reads=X[i:i + 2], key=OUTB[i // 2])
        for key_, ent in list(k.dma_sems.items()):
            k._wait(k.sp, (ent[0], ent[1]))
    return nc, tap_out


def make_in_maps(inputs, consts):
    maps = []
    shared = {n: np.ascontiguousarray(inputs[n], dtype=np.float32) for n in PARAM_SHAPES}
    for b in range(8):
        m = {"x": np.ascontiguousarray(inputs["x"][b], dtype=np.float32),
             "positions": np.ascontiguousarray(inputs["positions"][b], dtype=np.int32)}
        m.update(shared)
        m.update(consts)
        maps.append(m)
    return maps


_CACHE = {}


def kernel(**inputs):
    inputs = {n: np.asarray(v) for n, v in inputs.items()}
    if "nc" not in _CACHE:
        _CACHE["nc"] = build_program()[0]
    nc = _CACHE["nc"]
    maps = make_in_maps(inputs, host_consts())
    res = run_bass_kernel_spmd(nc, maps, core_ids=list(range(8)))
    return np.stack([np.asarray(r["out"]) for r in res.results], axis=0).astype(np.float32)
```

```python
import math
import numpy as np
from contextlib import ExitStack
import concourse.bass as bass
import concourse.mybir as mybir
from concourse.bass_utils import run_bass_kernel_spmd

F32 = mybir.dt.float32
BF16 = mybir.dt.bfloat16
I32 = mybir.dt.int32
AF = mybir.ActivationFunctionType
ALU = mybir.AluOpType
AX = mybir.AxisListType

S = 2048
D = 1024
NT = 16
DIN = 1932
DFF = 4096
L = 2
NCMP = 127
NEG = -30000.0
EPS = 1e-6
SCALE = 0.125
TWO_PI = 2.0 * math.pi


class Buf:
    __slots__ = ("name", "last_w", "reads", "excl")

    def __init__(self, name):
        self.name = name
        self.last_w = None
        self.reads = []
        self.excl = False


class Eng:
    def __init__(self, name, eng, sem, is_pe=False):
        self.name = name
        self.eng = eng
        self.sem = sem
        self.count = 0
        self.waited = {}
        self.is_pe = is_pe


class K:
    def __init__(self, nc, stack):
        self.nc = nc
        self.stack = stack
        mk = lambda n: stack.enter_context(nc.semaphore(n))
        self.pe = Eng("pe", nc.tensor, mk("s_pe"), is_pe=True)
        self.dve = Eng("dve", nc.vector, mk("s_dve"))
        self.act = Eng("act", nc.scalar, mk("s_act"))
        self.pool = Eng("pool", nc.gpsimd, mk("s_pool"))
        self.sp = Eng("sp", nc.sync, None)
        self.dma_sems = {}
        self.n_ops = 0
        self.n_waits = 0

    def buf(self, name):
        return Buf(name)

    def bufs(self, name, n):
        return [Buf(f"{name}{i}") for i in range(n)]

    def _wait(self, e, tok):
        sem, val = tok
        key = id(sem)
        if e.waited.get(key, 0) >= val:
            return
        e.eng.wait_ge(sem, val)
        e.waited[key] = val
        self.n_waits += 1

    def _deps(self, e, reads, writes):
        toks = []
        for b in reads:
            if b.last_w is not None:
                toks.append(b.last_w)
        for b in writes:
            if b.last_w is not None:
                toks.append(b.last_w)
            toks.extend(b.reads)
        for tok in toks:
            if e.is_pe and tok[0] is e.sem:
                continue
            self._wait(e, tok)

    def _commit(self, tok, reads, writes):
        for b in writes:
            b.last_w = tok
            b.reads = []
        for b in reads:
            if b not in writes:
                b.reads.append(tok)
                if len(b.reads) > 64:
                    b.reads = b.reads[-64:] if False else b.reads
        self.n_ops += 1

    def op(self, e, fn, reads=(), writes=()):
        reads = list(reads)
        writes = list(writes)
        for b in reads:
            if b.excl and b not in writes:
                writes.append(b)
        self._deps(e, reads, writes)
        inst = fn()
        e.count += 1
        inst.then_inc(e.sem, 1)
        tok = (e.sem, e.count)
        self._commit(tok, reads, writes)
        return tok

    def dma(self, out, in_, reads=(), writes=(), q=None, key=None, **kw):
        q = q or self.sp
        reads = list(reads)
        writes = list(writes)
        kb = key if key is not None else (writes[0] if writes else reads[0])
        if id(kb) not in self.dma_sems:
            s = self.stack.enter_context(self.nc.semaphore(f"d_{kb.name}"))
            self.dma_sems[id(kb)] = [s, 0]
        ent = self.dma_sems[id(kb)]
        self._deps(q, reads, writes)
        inst = q.eng.dma_start(out=out, in_=in_, **kw)
        ent[1] += 16
        inst.then_inc(ent[0], 16)
        tok = (ent[0], ent[1])
        self._commit(tok, reads, writes)
        return tok

    def finish(self, bufs, e=None):
        e = e or self.sp
        for b in bufs:
            if b.last_w is not None:
                self._wait(e, b.last_w)
            for t in b.reads:
                self._wait(e, t)


def _compact_reads(b):
    best = {}
    for sem, val in b.reads:
        kk = id(sem)
        if kk not in best or best[kk][1] < val:
            best[kk] = (sem, val)
    b.reads = list(best.values())


def host_consts():
    c = {}
    c["c_ident"] = np.eye(128, dtype=np.float32)
    t = np.arange(128)
    c["c_tril"] = (t[None, :] <= t[:, None]).astype(np.float32)
    c["c_triT"] = np.where(t[:, None] <= t[None, :], 0.0, NEG).astype(np.float32)
    c["c_tri2T"] = np.where(t[:, None] > t[None, :], 0.0, NEG).astype(np.float32)
    n = np.arange(NCMP)
    tt = np.arange(S)
    c["c_cmpbT"] = np.where(16 * n[:, None] + 31 <= tt[None, :], 0.0, NEG).astype(np.float32)
    j = np.arange(32)
    c["c_ebig"] = (j[:, None] == (tt[None, :] // 64)).astype(np.float32)
    ov = np.clip(np.minimum(16 * n[:, None] + 32, 64 * j[None, :] + 64) - np.maximum(16 * n[:, None], 64 * j[None, :]), 0, None) / 16.0
    c["c_overlap"] = ov.astype(np.float32)
    qb = tt // 64
    back = qb[:, None] - j[None, :]
    forced = (j[None, :] == 0) | ((back >= 0) & (back < 2))
    fut = back < 0
    A = np.where(forced | fut, 0.0, 1.0)
    Bm = np.where(forced, 1e9, np.where(fut, -1.0, 0.0))
    c["c_selA"] = A.astype(np.float32)
    c["c_selB"] = Bm.astype(np.float32)
    half = 8
    c["c_invf"] = (500000.0 ** (-np.arange(half, dtype=np.float32) * 2.0 / 16.0)).astype(np.float32)
    win = np.zeros((128, 2), np.float32)
    win[:64, 0] = 2; win[64:, 0] = 4; win[:64, 1] = 8; win[64:, 1] = 16
    c["c_invwin"] = (1.0 / win).astype(np.float32)
    tc = np.arange(16, dtype=np.float32)
    c["c_invcnt"] = (1.0 / np.minimum(tc[None, None, :] + 1.0, win[:, :, None])).astype(np.float32)
    return c


CONST_SHAPES = {
    "c_ident": [128, 128], "c_tril": [128, 128], "c_triT": [128, 128], "c_tri2T": [128, 128],
    "c_cmpbT": [NCMP, S], "c_ebig": [32, S], "c_overlap": [NCMP, 32], "c_selA": [S, 32], "c_selB": [S, 32],
    "c_invf": [8], "c_invwin": [128, 2], "c_invcnt": [128, 2, 16],
}

PARAM_SHAPES = {
    "pre_mix_norm": [L, D], "post_mix_norm": [L, D], "pre_ffn_norm": [L, D], "post_ffn_norm": [L, D],
    "w_in": [L, D, DIN], "conv_w": [L, 31, 256], "conv_b": [L, 256], "conv_ln_g": [L, 256], "conv_ln_b": [L, 256],
    "pool_w": [L, 4, 64, 64], "pool_scale": [L, 256], "sgu_ln_g": [L, 256], "sgu_ln_b": [L, 256],
    "sgu_w": [L, 4, 128, 128], "sgu_b": [L, 4, 128],
    "cmp_k_pe": [L, 32, 64], "cmp_k_w1": [L, 2048, 64], "cmp_k_w2": [L, 64, 64],
    "cmp_v_pe": [L, 32, 64], "cmp_v_w1": [L, 2048, 64], "cmp_v_w2": [L, 64, 64],
    "w_out": [L, D, D], "ffn_w1": [L, D, DFF], "ffn_w2": [L, DFF, D],
}


class _Stop(Exception):
    pass


def build_program(n_layers=L, taps=None, stop=None):
    taps = taps or []
    nc = bass.Bass("TRN2", target_bir_lowering=False)
    din = lambda n, s, dt=F32: nc.dram_tensor(n, list(s), dt, kind="ExternalInput").ap()
    x_d = din("x", [S, D])
    pos_d = din("positions", [S], I32)
    P = {n: din(n, s) for n, s in PARAM_SHAPES.items()}
    C = {n: din(n, s) for n, s in CONST_SHAPES.items()}
    out_d = nc.dram_tensor("out", [S, D], F32, kind="ExternalOutput").ap()
    tap_out = {}
    tap_keep = []

    with ExitStack() as st:
        k = K(nc, st)
        sb = lambda n, s, d=F32: st.enter_context(nc.sbuf_tensor(n, list(s), d))
        pe, dve, act, pool = k.pe, k.dve, k.act, k.pool
        V, A_, T, G = nc.vector, nc.scalar, nc.tensor, nc.gpsimd

        banks = [st.enter_context(nc.psum_tensor(f"ps{i}", [128, 512], F32)) for i in range(8)]
        BK = k.bufs("bank", 8)
        for b_ in BK:
            b_.excl = True
        ps_state = {"i": 0, "reserved": set()}

        def nextbank():
            while True:
                b = ps_state["i"] % 8
                ps_state["i"] += 1
                if b not in ps_state["reserved"]:
                    return b

        def bfv(b):
            return banks[b][:].bitcast(BF16)

        x_sb = sb("x_sb", [128, NT, D]); X = k.bufs("x", NT)
        RA = sb("RA", [128, 16384], BF16)
        RB = sb("RB", [128, 26688], BF16)
        RW = sb("RW", [128, 6144], BF16)
        identf = sb("identf", [128, 128]); identb = sb("identb", [128, 128], BF16)
        IDF = k.buf("idf"); IDB = k.buf("idb")
        triT = sb("triT", [128, 2, 128], BF16); TRI = k.buf("tri")
        ebig = sb("ebig", [128, S], BF16); EBIG = k.buf("ebig")
        selA = sb("selA", [128, NT, 32], BF16); selB = sb("selB", [128, NT, 32], BF16); SELAB = k.buf("selab")
        invf = sb("invf", [128, 8]); INVF = k.buf("invf")
        invwin = sb("invwin", [128, 2]); invcnt = sb("invcnt", [128, 2, 16]); INVW = k.buf("invw")
        cosT = sb("cosT", [128, NT, 8]); sinT = sb("sinT", [128, NT, 8]); CS = k.buf("cs")
        cosC = sb("cosC", [128, 8]); sinC = sb("sinC", [128, 8]); CSC = k.buf("csc")
        vaug = sb("vaug", [128, NT, 2, 65], BF16); VAUG = k.bufs("vaug", NT)
        vcmp = sb("vcmp", [128, 97], BF16); VCMP = k.buf("vcmp")
        onesb = sb("onesb", [128, 128], BF16); ONES = k.buf("ones")
        zerob = sb("zerob", [128, 272], BF16); ZERO = k.buf("zero")
        gsig = sb("gsig", [128, NT, 12]); GSIG = k.bufs("gsig", NT)
        gcol = sb("gcol", [128, 2, 8]); GCOL = k.buf("gcol")
        gpost = sb("gpost", [128, D]); GPOST = k.buf("gpost")
        convwT = sb("convwT", [128, 2, 31]); CONVW = k.buf("convw")
        cvcol = sb("cvcol", [128, 6]); CVCOL = k.buf("cvcol")
        bc5 = sb("bc5", [128, 5, 256]); BC5 = k.buf("bc5")
        poolw = sb("poolw", [128, 2, 128], BF16); POOLW = k.buf("poolw")
        poolsc = sb("poolsc", [128, 2]); POOLSC = k.buf("poolsc")
        sguwT = sb("sguwT", [128, 4, 128], BF16); SGUWT = k.buf("sguwT")
        sgub = sb("sgub", [1, 4, 128], BF16); SGUB = k.buf("sgub")
        w1kv = sb("w1kv", [128, 32, 64], BF16); W1KV = k.buf("w1kv")
        w2kv = sb("w2kv", [64, 2, 64], BF16); W2KV = k.buf("w2kv")
        peT = sb("peT", [128, 32], BF16); PET = k.buf("pet")
        cbias = sb("cbias", [64, 2]); CBIAS = k.buf("cbias")
        ss = sb("ss", [128, NT, 2]); SSQ = k.bufs("ss", NT)
        rstd = sb("rstd", [128, NT]); RSTD = k.bufs("rstd", NT)
        tmpA = sb("tmpA", [128, 4, 544]); TMPA = k.bufs("tmpA", 4)
        small = sb("small", [128, 4, 64]); SMALL = k.bufs("small", 4)
        posf = sb("posf", [128, NT]); POS = k.buf("pos")
        pocf = sb("pocf", [128, 1]); POSC = k.buf("posc")
        angfull = sb("angfull", [128, NT, 8]); ANGF = k.buf("angf")
        angc = sb("angc", [128, 8]); ANGC = k.buf("angc")
        epsb = sb("epsb", [128, 1]); EPSB = k.buf("epsb")
        imp = sb("imp", [128, 4, 32]); IMP = k.buf("imp")
        impm = sb("impm", [128, 4, 32]); IMPM = k.buf("impm")
        top8 = sb("top8", [128, 4, 8]); TOP8 = k.buf("top8"); TOP8S = k.bufs("top8s", 4)
        selb = sb("selb", [128, 4, 32], BF16); SELB = k.buf("selb"); SELBS = k.bufs("selbs", 4)
        fin = sb("fin", [128, 4, 12]); FIN = k.bufs("fin", 4)

        ang = tmpA[:, 1, 0:128].rearrange("p (i j) -> p i j", i=NT)
        angk = tmpA[:, 2, 0:128].rearrange("p (i j) -> p i j", i=NT)
        angi = tmpA[:, 3, 0:128].bitcast(I32).rearrange("p (i j) -> p i j", i=NT)
        ANG = k.buf("ang")
        sguwf = tmpA[:, 0, 0:512].rearrange("p (h s) -> p h s", h=4)
        SGUWF = TMPA[0]
        hT = RA[:, 0:16384].rearrange("p (c t) -> p c t", c=8)
        HT = [[k.buf(f"hT{c}_{g}") for g in range(4)] for c in range(8)]
        HTALL = [HT[c][g] for c in range(8) for g in range(4)]
        convoT = RA[:, 0:4096].rearrange("p (c t) -> p c t", c=2)
        CONVO = k.bufs("convo", NT)
        diag = RA[:, 4096:4096 + 7936].rearrange("p (c k m) -> p c k m", c=2, k=31)
        DIAG = k.buf("diag")
        DIAGS = [k.bufs(f"diag{c}_", 31) for c in range(2)]
        DIAGALL = DIAGS[0] + DIAGS[1]
        chb = RA[:, 12032:14080].rearrange("p (a t) -> p a t", a=4)
        CHB = k.bufs("chb", 4)
        PTb = RA[:, 4096:6144].rearrange("p (n t) -> p n t", n=4)
        PTB = k.bufs("PT", 4)
        selbT = RA[:, 6144:8192]
        SELBT = k.bufs("selbT", 4)
        oacc = RA[:, 8192:10240].bitcast(F32).rearrange("p (i c) -> p i c", i=4)
        OACC = k.buf("oacc")
        OACCH = k.bufs("oacch", 4)
        oacc2 = RA[:, 13312:15360].bitcast(F32).rearrange("p (i c) -> p i c", i=4)
        OACCH2 = k.bufs("oacch2", 4)
        oaccs = [oacc, oacc2]; OACCHS = [OACCH, OACCH2]
        PTc = RA[:, 15648:16160]; PTC = k.buf("PTc")
        nsab = RA[:, 10240:11264].rearrange("p (i c) -> p i c", i=4)
        NSAB = k.buf("nsab")
        hidkv = RA[0:64, 11264:11520].rearrange("p (a n) -> p a n", a=2)
        HIDKV = k.buf("hidkv")
        kcmpT = RA[:, 15392:15648].rearrange("p (a n) -> p a n", a=2)
        KCMPT = k.buf("kcmpT")
        kcr = RA[:, 11648:11776]
        KCR = k.buf("kcr")
        ftmp = RA[:, 11776:11776 + 512].bitcast(F32).rearrange("p (i c) -> p i c", i=4)
        FTMP = k.buf("ftmp")
        kvcR = RA[:, 13312:13312 + 2080].rearrange("p (r m) -> p r m", r=16)
        KVCR = k.buf("kvcR")
        NSATMP = PTB + SELBT + OACCH + OACCH2 + [OACC, NSAB, HIDKV, KCMPT, KCR, FTMP, KVCR, PTC]
        woutA = RW[:, 0:6144].rearrange("p (c n) -> p c n", c=6)
        woutB = RA[:, 4096:6144].rearrange("p (c n) -> p c n", c=2)
        WOUT = k.buf("wout"); WOUTA = k.buf("woutA")
        def wout_c(c):
            return woutA[:, c, :] if c < 6 else woutB[:, c - 6, :]
        junkD = RA[:, 12288:13312]; JUNKD = k.buf("junkD")
        hT2 = RA[:, 0:4096].rearrange("p (c t) -> p c t", c=8)
        HT2 = k.bufs("hT2", 8)
        w2s = RA[:, 4096:10240].rearrange("p (b c n) -> p b c n", b=3, c=4)
        W2S = k.bufs("w2s", 3)
        hnF = RA[:, 10240:14336].rearrange("p (a n) -> p a n", a=4); HNF = k.bufs("hnF", 4)
        junkF = RA[:, 14336:15360]; JUNKF = k.buf("junkF")
        o = 0
        def carve(n_el):
            nonlocal o
            v = RB[:, o:o + n_el]
            o += n_el
            return v
        gT = carve(2 * 2080).rearrange("p (c t) -> p c t", c=2)
        mixedT = carve(2 * 2048).rearrange("p (c t) -> p c t", c=2)
        uT = carve(2 * 2048).rearrange("p (c t) -> p c t", c=2)
        vtok = carve(NT * 256).rearrange("p (i c) -> p i c", i=NT)
        kvcT = carve(2048)
        qT = carve(2 * 2048).rearrange("p (c t) -> p c t", c=2)
        kT_off = o
        kT = carve(2 * 2048).rearrange("p (c t) -> p c t", c=2)
        assert o == 26688, o
        hnM = RB[:, kT_off:kT_off + 3072].rearrange("p (a n) -> p a n", a=3); HNM = k.bufs("hnM", 3)
        junkM = RB[:, kT_off + 3072:kT_off + 4096]; JUNKM = k.buf("junkM")
        kThi = RB[:, 0:4096].rearrange("p (c t) -> p c t", c=2)
        KTHI = k.buf("kThi")
        GT = [[k.buf(f"gT{c}_{g}") for g in range(4)] for c in range(2)]
        GPAD = k.buf("gpad")
        MIXED = [[k.buf(f"mx{c}_{g}") for g in range(4)] for c in range(2)]
        UT = [[k.buf(f"uT{c}_{g}") for g in range(4)] for c in range(2)]
        VTOK = k.bufs("vtok", NT)
        KVC = k.bufs("kvc", 4)
        QT = k.bufs("qT", NT)
        KT = k.bufs("kT", NT)
        yD = [RB[:, 12352 + 2048 * n_:12352 + 2048 * (n_ + 1)].bitcast(F32) for n_ in range(3)] + [RB[:, kT_off:kT_off + 2048].bitcast(F32)]
        YD = k.bufs("yD", 4)
        RBMIX = [b_ for r_ in GT + MIXED + UT for b_ in r_] + [GPAD, KTHI] + VTOK + KVC + QT + KT + YD
        fT = RB[:, 0:16384].rearrange("p (c t) -> p c t", c=32)
        FT = k.bufs("fT", 32)
        ybuf = RB[:, 16384:24576].bitcast(F32).rearrange("p (i d) -> p i d", i=4)
        YB = k.bufs("yb", 4)
        winA = RW[:, 0:4096].rearrange("p (c n) -> p c n", c=8)
        WINA = k.buf("winA")
        winS = RW[:, 0:6144].rearrange("p (b c n) -> p b c n", b=6, c=8)
        WINS = k.bufs("winS", 6)
        w1s = RW[:, 0:6144].rearrange("p (b c n) -> p b c n", b=3, c=8)
        W1S = k.bufs("w1s", 3)

        def interleave(gens, width, extra=()):
            gens = list(gens)
            extra = list(extra)
            active = []
            while gens or active or extra:
                while gens and len(active) < width:
                    active.append(gens.pop(0))
                for g_ in list(active) + list(extra):
                    try:
                        next(g_)
                    except StopIteration:
                        (active if g_ in active else extra).remove(g_)

        def drain(gen):
            for _ in gen:
                pass

        def handoff(old, new):
            best = {}
            for b_ in old:
                toks = list(b_.reads)
                if b_.last_w is not None:
                    toks.append(b_.last_w)
                for sem, val in toks:
                    kk = id(sem)
                    if kk not in best or best[kk][1] < val:
                        best[kk] = (sem, val)
            for b_ in new:
                b_.last_w = None
                b_.reads = list(best.values())

        def tap(name, ap, reads):
            if name not in taps:
                return
            t = nc.dram_tensor("tap_" + name, list(ap.shape), ap.dtype, kind="ExternalOutput").ap()
            tap_out[name] = t
            tb_ = k.buf("tap_" + name)
            tap_keep.append(tb_)
            k.dma(t, ap, reads=list(reads), key=tb_)

        try:
            k.dma(identf[:], C["c_ident"], writes=[IDF])
            k.op(dve, lambda: V.tensor_copy(identb[:], identf[:]), reads=[IDF], writes=[IDB])
            xv = x_d.rearrange("(i p) d -> p i d", p=128)
            for i in range(0, NT, 2):
                k.dma(x_sb[:, i:i + 2, :], xv[:, i:i + 2, :], writes=X[i:i + 2])

            k.dma(triT[:, 0, :], C["c_triT"], writes=[TRI], q=pool)
            k.dma(triT[:, 1, :], C["c_tri2T"], writes=[TRI], q=pool)
            k.op(dve, lambda: V.memset(ebig[:], 0.0), writes=[EBIG])
            k.dma(ebig[0:32, :], C["c_ebig"], reads=[EBIG], writes=[EBIG], q=pool)
            k.dma(selA[:], C["c_selA"].rearrange("(i p) j -> p i j", p=128), writes=[SELAB], q=pool)
            k.dma(selB[:], C["c_selB"].rearrange("(i p) j -> p i j", p=128), writes=[SELAB], q=pool)
            k.dma(invf[:], C["c_invf"].partition_broadcast(128), writes=[INVF])
            k.dma(invwin[:], C["c_invwin"], writes=[INVW])
            k.dma(invcnt[:], C["c_invcnt"], writes=[INVW])
            k.op(dve, lambda: V.memset(onesb[:], 1.0), writes=[ONES])
            k.op(dve, lambda: V.memset(zerob[:], 0.0), writes=[ZERO])
            k.op(dve, lambda: V.memset(vaug[:, :, :, 64:65], 1.0), writes=VAUG)
            k.op(dve, lambda: V.memset(vcmp[:], 0.0), writes=[VCMP])
            k.op(dve, lambda: V.memset(vcmp[:, 64:65], 1.0), reads=[VCMP], writes=[VCMP])
            k.dma(vcmp[0:NCMP, 65:97], C["c_overlap"], writes=[VCMP], q=pool)
            k.op(dve, lambda: V.memset(poolw[:], 0.0), writes=[POOLW])
            k.op(dve, lambda: V.memset(epsb[:], EPS), writes=[EPSB])

            def setup_rope():
                def sincos(ang_ap, np_, three, cos_ap, sin_ap, RD, WR):
                    a2 = ang[0:np_] if three else ang[0:np_, 0, :]
                    k2 = angk[0:np_] if three else angk[0:np_, 0, :]
                    ki = angi[0:np_] if three else angi[0:np_, 0, :]
                    W_ = [ANG, TMPA[1], TMPA[2], TMPA[3]]
                    for which, dst in ((0, sin_ap), (1, cos_ap)):
                        sh = 0.0 if which == 0 else math.pi / 2
                        k.op(dve, lambda: V.tensor_scalar(k2, ang_ap, sh, 1.0 / TWO_PI, op0=ALU.add, op1=ALU.mult), reads=RD, writes=W_)
                        k.op(dve, lambda: V.tensor_copy(ki, k2), reads=W_, writes=W_)
                        k.op(dve, lambda: V.tensor_copy(k2, ki), reads=W_, writes=W_)
                        k.op(dve, lambda: V.scalar_tensor_tensor(a2, k2, -TWO_PI, ang_ap, op0=ALU.mult, op1=ALU.add), reads=RD + W_, writes=W_)
                        if sh:
                            k.op(dve, lambda: V.tensor_scalar(a2, a2, sh, None, op0=ALU.add), reads=W_, writes=W_)
                        k.op(dve, lambda: V.tensor_scalar(k2, a2, math.pi, -TWO_PI, op0=ALU.is_gt, op1=ALU.mult), reads=W_, writes=W_)
                        k.op(dve, lambda: V.tensor_tensor(a2, a2, k2, op=ALU.add), reads=W_, writes=W_)
                        k.op(dve, lambda: V.tensor_scalar(k2, a2, -math.pi, TWO_PI, op0=ALU.is_lt, op1=ALU.mult), reads=W_, writes=W_)
                        k.op(dve, lambda: V.tensor_tensor(a2, a2, k2, op=ALU.add), reads=W_, writes=W_)
                        k.op(dve, lambda: V.tensor_scalar(a2, a2, -3.1415925, 3.1415925, op0=ALU.max, op1=ALU.min), reads=W_, writes=W_)
                        k.op(act, lambda dst=dst: A_.activation(out=dst, in_=a2, func=AF.Sin), reads=W_, writes=WR)

                stg_i = tmpA[0:16, 0, 0:128].bitcast(I32)
                stg_f = tmpA[0:16, 0, 128:256]
                k.dma(stg_i, pos_d.rearrange("(i p) -> i p", p=128), writes=[TMPA[0]])
                k.op(dve, lambda: V.tensor_copy(stg_f, stg_i), reads=[TMPA[0]], writes=[TMPA[0]])
                b_ = nextbank()
                k.op(pe, lambda b_=b_: T.transpose(banks[b_][:, 0:16], stg_f, identf[0:16, 0:16]), reads=[TMPA[0], IDF], writes=[BK[b_]])
                k.op(dve, lambda b_=b_: V.tensor_copy(posf[:], banks[b_][:, 0:16]), reads=[BK[b_]], writes=[POS])
                k.op(dve, lambda: V.tensor_tensor(angfull[:], posf[:].unsqueeze(2).to_broadcast([128, NT, 8]),
                                                  invf[:].unsqueeze(1).to_broadcast([128, NT, 8]), op=ALU.mult),
                     reads=[POS, INVF], writes=[ANGF])
                sincos(angfull[:], 128, True, cosT[:], sinT[:], [ANGF], [CS])
                c_i = tmpA[0:NCMP, 0, 256:272].bitcast(I32)
                k.dma(c_i, pos_d[16:S].rearrange("(n r) -> n r", r=16), reads=[TMPA[0]], writes=[TMPA[0]])
                k.op(dve, lambda: V.memset(pocf[:], 0.0), writes=[POSC])
                k.op(dve, lambda: V.tensor_copy(pocf[0:NCMP, :], c_i[:, 15:16]), reads=[TMPA[0], POSC], writes=[POSC])
                k.op(dve, lambda: V.tensor_scalar(angc[:], invf[:], pocf[:, 0:1], None, op0=ALU.mult), reads=[POSC, INVF], writes=[ANGC])
                sincos(angc[:], 128, False, cosC[:], sinC[:], [ANGC], [CSC])
                tap("cos", cosT[:], [CS]); tap("sin", sinT[:], [CS]); tap("cosC", cosC[:], [CSC])

            def norm_stats(tiles, junkv, JUNKB):
                for i in tiles:
                    k.op(act, lambda i=i: A_.activation(out=junkv, in_=x_sb[:, i, :], func=AF.Square, accum_out=ss[:, i, 0:1]),
                         reads=[X[i]], writes=[JUNKB, SSQ[i]])
                    k.op(act, lambda i=i: A_.activation(out=ss[:, i, 1:2], in_=ss[:, i, 0:1], func=AF.Sqrt, scale=1.0 / D, bias=epsb[:, 0:1]),
                         reads=[SSQ[i], EPSB], writes=[SSQ[i]])
                    k.op(dve, lambda i=i: V.reciprocal(rstd[:, i:i + 1], ss[:, i, 1:2]), reads=[SSQ[i]], writes=[RSTD[i]])

            def rmsnorm_to_hT(which, tiles, dstT, DST, tcol0, hnv, HNB, junkv, JUNKB, stats=True):
                if stats:
                    norm_stats(tiles, junkv, JUNKB)
                if stop == "A0":
                    raise _Stop()
                hb = [nextbank() for _ in range(4)]
                for n_, i in enumerate(tiles):
                    hsel = n_ % len(HNB)
                    k.op(dve, lambda i=i, hsel=hsel: V.tensor_scalar(hnv[:, hsel, :], x_sb[:, i, :], rstd[:, i:i + 1], None, op0=ALU.mult),
                         reads=[X[i], RSTD[i]], writes=[HNB[hsel]])
                    def tr8(n_=n_, hsel=hsel):
                        r = None
                        for c in range(8):
                            dstp = bfv(hb[c // 2])[:, (c % 2) * 512 + n_ * 128:(c % 2) * 512 + (n_ + 1) * 128]
                            r = T.transpose(dstp, hnv[:, hsel, c * 128:(c + 1) * 128], identb[:])
                        return r
                    k.op(pe, tr8, reads=[HNB[hsel], IDB], writes=[BK[b_] for b_ in hb])
                if stop == "A1":
                    raise _Stop()
                for c in range(8):
                    b = hb[c // 2]
                    src = bfv(b)[:, (c % 2) * 512:(c % 2) * 512 + 512]
                    import os as _os
                    _ev = _os.environ.get("EV", "")
                    if _ev == "copy":
                        k.op(dve, lambda src=src, c=c: V.tensor_copy(dstT[:, c, tcol0:tcol0 + 512], src), reads=[BK[b], GCOL], writes=[DST[c]])
                    elif (c % 2 == 0 or _ev == "dve") and _ev != "act":
                        k.op(dve, lambda src=src, c=c: V.tensor_scalar(dstT[:, c, tcol0:tcol0 + 512], src, gcol[:, which, c:c + 1], None, op0=ALU.mult),
                             reads=[BK[b], GCOL], writes=[DST[c]])
                    else:
                        k.op(act, lambda src=src, c=c: A_.activation(out=dstT[:, c, tcol0:tcol0 + 512], in_=src, func=AF.Identity, scale=gcol[:, which, c:c + 1]),
                             reads=[BK[b], GCOL], writes=[DST[c]])

                if stop == "A2":
                    raise _Stop()

            def layernorm256(srcv, SRCB, ta, gi, bi, outv, OUTB):
                sm = small[:, ta, :]
                k.op(dve, lambda: V.bn_stats(sm[:, 0:6], srcv), reads=[SRCB], writes=[SMALL[ta]])
                yield
                k.op(dve, lambda: V.bn_aggr(sm[:, 8:10], sm[:, 0:6]), reads=[SMALL[ta]], writes=[SMALL[ta]])
                yield
                k.op(act, lambda: A_.activation(out=sm[:, 10:11], in_=sm[:, 9:10], func=AF.Sqrt, bias=epsb[:, 0:1]), reads=[SMALL[ta], EPSB], writes=[SMALL[ta]])
                yield
                k.op(dve, lambda: V.reciprocal(sm[:, 11:12], sm[:, 10:11]), reads=[SMALL[ta]], writes=[SMALL[ta]])
                yield
                k.op(dve, lambda: V.tensor_scalar(srcv, srcv, sm[:, 8:9], sm[:, 11:12], op0=ALU.subtract, op1=ALU.mult), reads=[SRCB, SMALL[ta]], writes=[SRCB])
                yield
                k.op(dve, lambda: V.tensor_tensor(srcv, srcv, bc5[:, gi, :], op=ALU.mult), reads=[SRCB, BC5], writes=[SRCB])
                yield
                k.op(dve, lambda: V.tensor_tensor(outv, srcv, bc5[:, bi, :], op=ALU.add), reads=[SRCB, BC5], writes=[OUTB] if OUTB is not SRCB else [SRCB])
                yield

            def rope(dst_lo, dst_hi, x1, x2, cb, sb_, t0, t1, RD, WR, TB):
                k.op(dve, lambda: V.tensor_tensor(t0, x1, cb, op=ALU.mult), reads=RD, writes=[TB])
                yield
                k.op(dve, lambda: V.tensor_tensor(t1, x2, sb_, op=ALU.mult), reads=RD + [TB], writes=[TB])
                yield
                k.op(dve, lambda: V.tensor_tensor(dst_lo, t0, t1, op=ALU.subtract), reads=[TB], writes=WR)
                yield
                k.op(dve, lambda: V.tensor_tensor(t0, x2, cb, op=ALU.mult), reads=RD + [TB], writes=[TB])
                yield
                k.op(dve, lambda: V.tensor_tensor(t1, x1, sb_, op=ALU.mult), reads=RD + [TB], writes=[TB])
                yield
                k.op(dve, lambda: V.tensor_tensor(dst_hi, t0, t1, op=ALU.add), reads=[TB], writes=WR)
                yield

            def load_col(dst, vec, B_):
                k.dma(dst, vec.rearrange("(c p) -> p c", p=128), writes=[B_], allow_slow_non_contiguous=True)

            first = True
            for l in range(n_layers):
                if not first:
                    handoff(HT2 + W2S + HNF + [JUNKD, JUNKF], HTALL)
                    handoff(FT + YB, RBMIX + HNM + [JUNKM])
                    handoff(W1S, [WINA] + WINS)
                else:
                    handoff([], HNM + [JUNKM])
                first = False
                stg = tmpA[:, 1, :]; stg2 = tmpA[:, 2, :]
                k.op(dve, lambda: V.memset(stg[0:32, 256:512], 0.0), writes=[TMPA[1]])
                k.dma(stg[0:8, 0:128], P["pre_mix_norm"][l].rearrange("(c p) -> c p", p=128), writes=[TMPA[1]])
                k.dma(stg[0:8, 128:256], P["pre_ffn_norm"][l].rearrange("(c p) -> c p", p=128), writes=[TMPA[1]])
                k.dma(stg[0:31, 256:512], P["conv_w"][l], writes=[TMPA[1]])
                k.dma(stg2[0:2, 0:128], P["pool_scale"][l].rearrange("(c p) -> c p", p=128), writes=[TMPA[2]])
                k.dma(stg2[0:32, 128:192], P["cmp_k_pe"][l], writes=[TMPA[2]])
                k.dma(stg2[0:32, 192:256], P["cmp_v_pe"][l], writes=[TMPA[2]])
                k.dma(stg2[0:2, 256:384], P["conv_b"][l].rearrange("(c p) -> c p", p=128), writes=[TMPA[2]])
                k.dma(stg2[2:4, 256:384], P["conv_ln_g"][l].rearrange("(c p) -> c p", p=128), writes=[TMPA[2]])
                k.dma(stg2[4:6, 256:384], P["conv_ln_b"][l].rearrange("(c p) -> c p", p=128), writes=[TMPA[2]])
                b_ = nextbank()
                for (src_, n_, c0_, TB_) in ((stg[0:8, 0:128], 8, 0, TMPA[1]), (stg[0:8, 128:256], 8, 8, TMPA[1]), (stg2[0:2, 0:128], 2, 16, TMPA[2]),
                                            (stg[0:32, 256:384], 32, 32, TMPA[1]), (stg[0:32, 384:512], 32, 64, TMPA[1]), (stg2[0:32, 128:256], 32, 96, TMPA[2]),
                                        (stg2[0:6, 256:384], 6, 128, TMPA[2])):
                    k.op(pe, lambda src_=src_, n_=n_, c0_=c0_, b_=b_: T.transpose(banks[b_][:, c0_:c0_ + n_], src_, identf[0:n_, 0:n_]), reads=[TB_, IDF], writes=[BK[b_]])
                k.op(dve, lambda b_=b_: V.tensor_copy(gcol[:].rearrange("p a c -> p (a c)"), banks[b_][:, 0:16]), reads=[BK[b_]], writes=[GCOL])
                k.op(dve, lambda b_=b_: V.tensor_copy(poolsc[:], banks[b_][:, 16:18]), reads=[BK[b_]], writes=[POOLSC])
                k.op(dve, lambda b_=b_: V.tensor_copy(convwT[:, 0, :], banks[b_][:, 32:63]), reads=[BK[b_]], writes=[CONVW])
                k.op(dve, lambda b_=b_: V.tensor_copy(convwT[:, 1, :], banks[b_][:, 64:95]), reads=[BK[b_]], writes=[CONVW])
                k.op(dve, lambda b_=b_: V.tensor_copy(peT[:], banks[b_][:, 96:128]), reads=[BK[b_]], writes=[PET])
                k.op(dve, lambda b_=b_: V.tensor_copy(cvcol[:], banks[b_][:, 128:134]), reads=[BK[b_]], writes=[CVCOL])
                for n_, nm in enumerate(["conv_b", "conv_ln_g", "conv_ln_b", "sgu_ln_g", "sgu_ln_b"]):
                    k.dma(bc5[:, n_, :], P[nm][l].partition_broadcast(128), writes=[BC5])
                for gi in range(4):
                    r0 = 64 * (gi % 2)
                    k.dma(poolw[r0:r0 + 64, gi // 2, r0:r0 + 64], P["pool_w"][l, gi], writes=[POOLW], q=pool)
                k.dma(sguwf, P["sgu_w"][l].rearrange("h t s -> t h s"), writes=[SGUWF])
                k.dma(sgub[:], P["sgu_b"][l:l + 1], writes=[SGUB], q=pool)
                k.dma(w1kv[0:64], P["cmp_k_w1"][l].rearrange("(l d) e -> d l e", d=64), writes=[W1KV], q=pool)
                k.dma(w1kv[64:128], P["cmp_v_w1"][l].rearrange("(l d) e -> d l e", d=64), writes=[W1KV], q=pool)
                k.dma(w2kv[:, 0, :], P["cmp_k_w2"][l], writes=[W2KV], q=pool)
                k.dma(w2kv[:, 1, :], P["cmp_v_w2"][l], writes=[W2KV], q=pool)
                k.op(pool, lambda: G.affine_select(out=sguwf, in_=sguwf, pattern=[[0, 4], [-1, 128]], compare_op=ALU.is_ge,
                                                   fill=0.0, base=0, channel_multiplier=1), reads=[SGUWF], writes=[SGUWF])
                b = nextbank()
                for h in range(4):
                    k.op(pe, lambda h=h, b=b: T.transpose(banks[b][:, h * 128:(h + 1) * 128], sguwf[:, h, :], identf[:]),
                         reads=[SGUWF, IDF], writes=[BK[b]])
                k.op(dve, lambda b=b: V.tensor_copy(sguwT[:].rearrange("p h t -> p (h t)"), banks[b][:, 0:512]), reads=[BK[b]], writes=[SGUWT])

                if stop == "S2":
                    break
                w_in_v = P["w_in"][l].rearrange("(c p) n -> p c n", p=128)
                for c in range(8):
                    k.dma(winA[:, c, :], w_in_v[:, c, 1024:1536], writes=[WINA], q=pool)
                for m_ in range(2):
                    k.dma(winS[:, 4 + m_], w_in_v[:, :, [256, 0][m_]:[256, 0][m_] + 128], writes=[WINS[4 + m_]], q=pool)

                norm_stats(list(range(NT)), junkM, JUNKM)
                for g in range(4):
                    rmsnorm_to_hT(0, list(range(4 * g, 4 * g + 4)), hT, [HT[c][g] for c in range(8)], g * 512, hnM, HNM, junkM, JUNKM, stats=False)
                if l == 0:
                    setup_rope()
                handoff(HNM + [JUNKM], KT)
                if l == 0:
                    tap("hT", hT, HTALL)
                if stop == "A":
                    break

                def b1_body(i):
                    g = i // 4
                    b = nextbank()
                    def mmg(b=b, i=i):
                        r = None
                        for c in range(8):
                            r = T.matmul(banks[b][:, 0:512], lhsT=hT[:, c, i * 128:(i + 1) * 128], rhs=winA[:, c, :], start=(c == 0), stop=(c == 7))
                        return r
                    k.op(pe, mmg, reads=[HT[c][g] for c in range(8)] + [WINA], writes=[BK[b]])
                    yield
                    ta = i % 4
                    sv = tmpA[:, ta, 0:256]
                    k.op(act, lambda b=b, sv=sv: A_.activation(out=sv, in_=banks[b][:, 0:256], func=AF.Gelu_apprx_tanh), reads=[BK[b]], writes=[TMPA[ta]])
                    yield from layernorm256(sv, TMPA[ta], ta, 3, 4, vtok[:, i, :], VTOK[i])
                    qr = tmpA[:, ta, 256:384].bitcast(BF16)
                    qps = banks[b][:, 256:512].rearrange("p (h d) -> p h d", h=4)
                    qrv = qr.rearrange("p (h d) -> p h d", h=4)
                    cb = cosT[:, i:i + 1, :].to_broadcast([128, 4, 8]); sb_ = sinT[:, i:i + 1, :].to_broadcast([128, 4, 8])
                    tq = tmpA[:, ta, 384:448].rearrange("p (a h d) -> p a h d", a=2, h=4)
                    k.op(act, lambda qr=qr, b=b: A_.copy(qr, banks[b][:, 256:512]), reads=[BK[b]], writes=[TMPA[ta]])
                    yield from rope(qrv[:, :, 0:8], qrv[:, :, 8:16], qps[:, :, 0:8], qps[:, :, 8:16], cb, sb_, tq[:, 0], tq[:, 1], [BK[b], CS], [TMPA[ta]], TMPA[ta])
                    b2 = nextbank()
                    k.op(pe, lambda b2=b2, qr=qr: (T.transpose(bfv(b2)[:, 0:128], qr[:, 0:128], identb[:]), T.transpose(bfv(b2)[:, 128:256], qr[:, 128:256], identb[:]))[1],
                         reads=[TMPA[ta], IDB], writes=[BK[b2]])
                    k.op(act, lambda b2=b2, i=i: A_.copy(qT[:, :, i * 128:(i + 1) * 128], bfv(b2)[:, 0:256].rearrange("p (c t) -> p c t", c=2)),
                         reads=[BK[b2]], writes=[QT[i]])
                interleave([b1_body(i) for i in range(NT)], 4)
                if l == 0:
                    tap("vtok", vtok, VTOK); tap("qT", qT, QT)
                if stop == "B1":
                    break

                for c in range(8):
                    k.dma(winA[:, c, 0:268], w_in_v[:, c, 1664:1932], writes=[WINA], q=pool)
                def b2_body(i):
                    g = i // 4
                    b = nextbank()
                    def mmg2(b=b, i=i):
                        r = None
                        for c in range(8):
                            r = T.matmul(banks[b][:, 0:268], lhsT=hT[:, c, i * 128:(i + 1) * 128], rhs=winA[:, c, 0:268], start=(c == 0), stop=(c == 7))
                        return r
                    k.op(pe, mmg2, reads=[HT[c][g] for c in range(8)] + [WINA], writes=[BK[b]])
                    yield
                    ta = i % 4
                    v4 = banks[b][:, 0:256].rearrange("p (h d) -> p h d", h=4)
                    k.op(act, lambda i=i, v4=v4: A_.copy(vaug[:, i, :, 0:64], v4[:, 1::2, :]), reads=[BK[b]], writes=[VAUG[i]])
                    k.op(act, lambda i=i, b=b: A_.activation(out=gsig[:, i, :], in_=banks[b][:, 256:268], func=AF.Sigmoid), reads=[BK[b]], writes=[GSIG[i]])
                    yield
                    kr = tmpA[:, ta, 0:128].bitcast(BF16).rearrange("p (a d) -> p a d", a=2)
                    tq = tmpA[:, ta, 384:416].rearrange("p (a h d) -> p a h d", a=2, h=2)
                    cb = cosT[:, i:i + 1, :].to_broadcast([128, 2, 8]); sb_ = sinT[:, i:i + 1, :].to_broadcast([128, 2, 8])
                    k.op(act, lambda kr=kr, v4=v4: A_.copy(kr[:, :, 0:64], v4[:, 0::2, :]), reads=[BK[b]], writes=[TMPA[ta]])
                    yield from rope(kr[:, :, 0:8], kr[:, :, 8:16], v4[:, 0::2, 0:8], v4[:, 0::2, 8:16], cb, sb_, tq[:, 0], tq[:, 1], [BK[b], CS], [TMPA[ta]], TMPA[ta])
                    k.op(dve, lambda kr=kr: V.tensor_copy(kr[:, :, 64:128], kr[:, :, 0:64]), reads=[TMPA[ta]], writes=[TMPA[ta]])
                    yield
                    b2 = nextbank()
                    k.op(pe, lambda b2=b2, kr=kr: (T.transpose(bfv(b2)[:, 0:128], kr[:, 0, :], identb[:]), T.transpose(bfv(b2)[:, 128:256], kr[:, 1, :], identb[:]))[1],
                         reads=[TMPA[ta], IDB], writes=[BK[b2]])
                    k.op(act, lambda b2=b2, i=i: A_.copy(kT[:, :, i * 128:(i + 1) * 128], bfv(b2)[:, 0:256].rearrange("p (c t) -> p c t", c=2)),
                         reads=[BK[b2]], writes=[KT[i]])
                interleave([b2_body(i) for i in range(NT)], 4)
                if l == 0:
                    tap("kT", kT, KT); tap("vaug", vaug[:], VAUG); tap("gsig", gsig[:], GSIG)
                if stop == "B2":
                    break

                B3_COLS = [256, 0, 384, 128, 512, 640, 768, 896, 1536]
                def ws_slot(n_):
                    return (4 + n_) % 6
                ws_loaded = {"n": 2}
                def load_upto(n_):
                    while ws_loaded["n"] <= min(n_, 8):
                        m_ = ws_loaded["n"]
                        k.dma(winS[:, ws_slot(m_)], w_in_v[:, :, B3_COLS[m_]:B3_COLS[m_] + 128], writes=[WINS[ws_slot(m_)]], q=pool)
                        ws_loaded["n"] += 1
                handoff([WINA], WINS[0:4])
                load_upto(5)

                def fm_mm(slot, g):
                    b = nextbank()
                    def f(b=b):
                        r = None
                        for c in range(8):
                            r = T.matmul(banks[b][:, 0:512], lhsT=winS[:, slot, c, :], rhs=hT[:, c, g * 512:(g + 1) * 512], start=(c == 0), stop=(c == 7))
                        return r
                    k.op(pe, f, reads=[HT[c][g] for c in range(8)] + [WINS[slot]], writes=[BK[b]])
                    return b

                k.op(dve, lambda: V.memset(gT[:, :, 0:32], 0.0), writes=[GPAD])
                sgT = RB[:, 16448:18496].bitcast(F32).rearrange("p (a n) -> p a n", a=2)
                SGT = k.bufs("sgT", 2)
                handoff(KVC, SGT)
                def convpair_stream():
                    for ch in range(2):
                        for g in range(4):
                            bg = fm_mm(ws_slot(2 * ch), g)
                            bv = fm_mm(ws_slot(2 * ch + 1), g)
                            ta = g % 2
                            sg = sgT[:, ta, :]
                            k.op(act, lambda sg=sg, bg=bg: A_.activation(out=sg, in_=banks[bg][:, 0:512], func=AF.Sigmoid), reads=[BK[bg]], writes=[SGT[ta]])
                            k.op(dve, lambda sg=sg, bv=bv, ch=ch, g=g: V.tensor_tensor(gT[:, ch, 32 + g * 512:32 + (g + 1) * 512], banks[bv][:, 0:512], sg, op=ALU.mult),
                                 reads=[BK[bv], SGT[ta]], writes=[GT[ch][g]])
                            yield
                def pool_stream():
                    for ch in range(2):
                        for g in range(4):
                            bp = fm_mm(ws_slot(4 + ch), g)
                            Pn = tmpA[:, g % 2, 0:528]; PB = TMPA[g % 2]
                            T1 = tmpA[:, 2, 0:528]; T2 = tmpA[:, 3, 0:528]
                            if g == 0:
                                k.op(dve, lambda Pn=Pn: V.memset(Pn[:, 0:16], 0.0), writes=[PB])
                            else:
                                Pp = tmpA[:, (g - 1) % 2, 0:528]
                                k.op(dve, lambda Pn=Pn, Pp=Pp: V.tensor_copy(Pn[:, 0:16], Pp[:, 512:528]), reads=[TMPA[(g - 1) % 2]], writes=[PB])
                            k.op(act, lambda Pn=Pn, bp=bp: A_.copy(Pn[:, 16:528], banks[bp][:, 0:512]), reads=[BK[bp]], writes=[PB])
                            yield
                            k.op(dve, lambda Pn=Pn: V.tensor_tensor(T1[:, 1:528], Pn[:, 1:528], Pn[:, 0:527], op=ALU.add), reads=[PB], writes=[TMPA[2]])
                            if ch == 0:
                                k.op(dve, lambda: V.tensor_tensor(T2[64:128, 3:528], T1[64:128, 3:528], T1[64:128, 1:526], op=ALU.add), reads=[TMPA[2]], writes=[TMPA[3]])
                            else:
                                k.op(dve, lambda: V.tensor_tensor(T2[:, 3:528], T1[:, 3:528], T1[:, 1:526], op=ALU.add), reads=[TMPA[2]], writes=[TMPA[3]])
                                k.op(dve, lambda: V.tensor_tensor(T1[:, 7:528], T2[:, 7:528], T2[:, 3:524], op=ALU.add), reads=[TMPA[3]], writes=[TMPA[2]])
                                k.op(dve, lambda: V.tensor_tensor(T2[64:128, 15:528], T1[64:128, 15:528], T1[64:128, 7:520], op=ALU.add), reads=[TMPA[2]], writes=[TMPA[3]])
                            for (r0, Tt, TB_) in ((0, T1, TMPA[2]), (64, T2, TMPA[3])):
                                k.op(dve, lambda r0=r0, Tt=Tt, Pn=Pn, ch=ch, g=g: V.scalar_tensor_tensor(
                                    mixedT[r0:r0 + 64, ch, g * 512:(g + 1) * 512], Tt[r0:r0 + 64, 16:528], invwin[r0:r0 + 64, ch:ch + 1], Pn[r0:r0 + 64, 16:528],
                                    op0=ALU.mult, op1=ALU.subtract), reads=[TB_, PB, INVW], writes=[MIXED[ch][g]])
                                if g == 0:
                                    sm = small[r0:r0 + 64, 0, 0:16]
                                    k.op(dve, lambda r0=r0, Tt=Tt, sm=sm, ch=ch: V.tensor_tensor(sm, Tt[r0:r0 + 64, 16:32], invcnt[r0:r0 + 64, ch, :], op=ALU.mult),
                                         reads=[TB_, INVW], writes=[SMALL[0]])
                                    k.op(dve, lambda r0=r0, sm=sm, Pn=Pn, ch=ch: V.tensor_tensor(mixedT[r0:r0 + 64, ch, 0:16], sm, Pn[r0:r0 + 64, 16:32], op=ALU.subtract),
                                         reads=[SMALL[0], PB], writes=[MIXED[ch][g]])
                def u_stream():
                    for ch in range(2):
                        for g in range(4):
                            bu = fm_mm(ws_slot(6 + ch), g)
                            k.op(act, lambda bu=bu, ch=ch, g=g: A_.activation(out=uT[:, ch, g * 512:(g + 1) * 512], in_=banks[bu][:, 0:512], func=AF.Gelu_apprx_tanh),
                                 reads=[BK[bu]], writes=[UT[ch][g]])
                            yield
                    for g in range(4):
                        bk_ = fm_mm(ws_slot(8), g)
                        k.op(dve, lambda bk_=bk_, g=g: V.tensor_copy(kvcT[:, g * 512:(g + 1) * 512], banks[bk_][:, 0:512]), reads=[BK[bk_]], writes=[KVC[g]])
                        yield
                interleave([pool_stream(), convpair_stream()], 2)
                handoff(SGT, KVC)
                load_upto(8)
                drain(u_stream())
                if l == 0:
                    tap("gT", gT, GT[0] + GT[1] + [GPAD]); tap("mixedT", mixedT, MIXED[0] + MIXED[1]); tap("uT", uT, UT[0] + UT[1]); tap("kvcT", kvcT, KVC)
                if stop == "B3":
                    break

                handoff(HTALL, CONVO + DIAGALL + CHB)
                handoff([WINA] + WINS, [WOUTA])
                w_out_v = P["w_out"][l].rearrange("(c p) n -> p c n", p=128)
                for c in range(6):
                    k.dma(woutA[:, c, :], w_out_v[:, c, :], writes=[WOUTA], q=pool)
                for ch in range(2):
                    for kk in range(31):
                        if kk % 2 == 0:
                            k.op(dve, lambda ch=ch, kk=kk: V.tensor_scalar(diag[:, ch, kk, :], identf[:], convwT[:, ch, kk:kk + 1], None, op0=ALU.mult),
                                 reads=[IDF, CONVW], writes=[DIAGS[ch][kk]])
                        else:
                            k.op(act, lambda ch=ch, kk=kk: A_.activation(out=diag[:, ch, kk, :], in_=identf[:], func=AF.Identity, scale=convwT[:, ch, kk:kk + 1]),
                                 reads=[IDF, CONVW], writes=[DIAGS[ch][kk]])
                def conv_stream():
                    for g in range(4):
                        hf = [tmpA[:, 0, 0:512], tmpA[:, 1, 0:512]]
                        mean = tmpA[:, 2, 0:512]; var = tmpA[:, 3, 0:512]
                        glist = sorted(set([max(0, (g * 512 - 30)) // 512, g]))
                        for ch in range(2):
                            b = nextbank()
                            def cmm(b=b, g=g, ch=ch):
                                r = None
                                for kk in range(31):
                                    r = T.matmul(banks[b][:, 0:512], lhsT=diag[:, ch, kk, :], rhs=gT[:, ch, 2 + kk + g * 512:2 + kk + g * 512 + 512],
                                                 start=(kk == 0), stop=(kk == 30))
                                return r
                            k.op(pe, cmm, reads=[GT[ch][g_] for g_ in glist] + [GPAD] + DIAGS[ch], writes=[BK[b]])
                            k.op(act, lambda b=b, ch=ch: A_.activation(out=hf[ch], in_=banks[b][:, 0:512], func=AF.Identity, bias=cvcol[:, ch:ch + 1]),
                                 reads=[BK[b], CVCOL], writes=[TMPA[ch]])
                            k.op(dve, lambda ch=ch: V.tensor_copy(chb[:, ch, :], hf[ch]), reads=[TMPA[ch]], writes=[CHB[ch]])
                            k.op(act, lambda ch=ch: A_.activation(out=chb[:, 2 + ch, :], in_=hf[ch], func=AF.Square), reads=[TMPA[ch]], writes=[CHB[2 + ch]])
                            yield
                        bs1 = nextbank(); bs2 = nextbank()
                        def smm1(bs1=bs1):
                            T.matmul(banks[bs1][:, 0:512], lhsT=onesb[:], rhs=chb[:, 0, :], start=True, stop=False)
                            return T.matmul(banks[bs1][:, 0:512], lhsT=onesb[:], rhs=chb[:, 1, :], start=False, stop=True)
                        def smm2_(bs2=bs2):
                            T.matmul(banks[bs2][:, 0:512], lhsT=onesb[:], rhs=chb[:, 2, :], start=True, stop=False)
                            return T.matmul(banks[bs2][:, 0:512], lhsT=onesb[:], rhs=chb[:, 3, :], start=False, stop=True)
                        k.op(pe, smm1, reads=[CHB[0], CHB[1], ONES], writes=[BK[bs1]])
                        k.op(pe, smm2_, reads=[CHB[2], CHB[3], ONES], writes=[BK[bs2]])
                        k.op(act, lambda bs1=bs1: A_.activation(out=mean, in_=banks[bs1][:, 0:512], func=AF.Identity, scale=1.0 / 256), reads=[BK[bs1]], writes=[TMPA[2]])
                        k.op(dve, lambda: V.tensor_tensor(var, mean, mean, op=ALU.mult), reads=[TMPA[2]], writes=[TMPA[3]])
                        k.op(dve, lambda bs2=bs2: V.scalar_tensor_tensor(var, banks[bs2][:, 0:512], 1.0 / 256, var, op0=ALU.mult, op1=ALU.subtract),
                             reads=[BK[bs2], TMPA[3]], writes=[TMPA[3]])
                        k.op(act, lambda: A_.activation(out=var, in_=var, func=AF.Sqrt, bias=epsb[:, 0:1]), reads=[TMPA[3], EPSB], writes=[TMPA[3]])
                        k.op(dve, lambda: V.reciprocal(var, var), reads=[TMPA[3]], writes=[TMPA[3]])
                        yield
                        for ch in range(2):
                            k.op(dve, lambda ch=ch: V.tensor_tensor(hf[ch], hf[ch], mean, op=ALU.subtract), reads=[TMPA[ch], TMPA[2]], writes=[TMPA[ch]])
                            k.op(dve, lambda ch=ch: V.tensor_tensor(hf[ch], hf[ch], var, op=ALU.mult), reads=[TMPA[ch], TMPA[3]], writes=[TMPA[ch]])
                            k.op(act, lambda ch=ch, g=g: A_.activation(out=convoT[:, ch, g * 512:(g + 1) * 512], in_=hf[ch], func=AF.Silu,
                                                                        scale=cvcol[:, 2 + ch:3 + ch], bias=cvcol[:, 4 + ch:5 + ch]),
                                 reads=[TMPA[ch], CVCOL], writes=CONVO[4 * g:4 * g + 4])
                            yield
                def plsgu_stream():
                    for ch in range(2):
                        for g in range(4):
                            b = nextbank()
                            k.op(pe, lambda b=b, ch=ch, g=g: T.matmul(banks[b][:, 0:512], lhsT=poolw[:, ch, :], rhs=mixedT[:, ch, g * 512:(g + 1) * 512], start=True, stop=True),
                                 reads=[POOLW, MIXED[ch][g]], writes=[BK[b]])
                            k.op(act, lambda b=b, ch=ch, g=g: A_.activation(out=mixedT[:, ch, g * 512:(g + 1) * 512], in_=banks[b][:, 0:512], func=AF.Identity, scale=poolsc[:, ch:ch + 1]),
                                 reads=[BK[b], POOLSC], writes=[MIXED[ch][g]])
                            yield
                    for pr in range(2):
                        for g in range(4):
                            b = nextbank()
                            def smm(b=b, pr=pr, g=g):
                                r = None
                                for ti in range(4):
                                    i = 4 * g + ti
                                    for hh in range(2):
                                        h = 2 * pr + hh
                                        o_ = banks[b][64 * hh:64 * hh + 64, ti * 128:(ti + 1) * 128]
                                        T.matmul(o_, lhsT=vtok[:, i, h * 64:(h + 1) * 64], rhs=sguwT[:, h, :], start=True, stop=False)
                                        r = T.matmul(o_, lhsT=onesb[0:1, 0:64], rhs=sgub[0:1, h, :], start=False, stop=True)
                                return r
                            k.op(pe, smm, reads=VTOK[4 * g:4 * g + 4] + [SGUWT, SGUB, ONES], writes=[BK[b]])
                            k.op(dve, lambda b=b, pr=pr, g=g: V.tensor_tensor(uT[:, pr, g * 512:(g + 1) * 512], banks[b][:, 0:512], uT[:, pr, g * 512:(g + 1) * 512], op=ALU.mult),
                                 reads=[BK[b], UT[pr][g]], writes=[UT[pr][g]])
                            yield
                interleave([conv_stream()], 1, extra=[plsgu_stream()])
                if l == 0:
                    tap("convoT", convoT, CONVO)
                if stop == "C1":
                    break
                handoff(DIAGALL + CHB, NSATMP)
                k.op(dve, lambda: V.memset(selbT[:, :], 0.0), writes=SELBT)
                handoff(GT[0] + GT[1] + [GPAD], [KTHI])
                k.op(dve, lambda: V.memset(kThi[0:64], 0.0), writes=[KTHI])
                k.op(act, lambda: A_.copy(kThi[64:128], kT[64:128]), reads=KT + [KTHI], writes=[KTHI])
                k.op(dve, lambda: V.memset(kT[64:128], 0.0), reads=[KTHI], writes=KT)

                if l == 0:
                    tap("poolT", mixedT, MIXED[0] + MIXED[1]); tap("sguT", uT, UT[0] + UT[1])
                if stop == "C3":
                    break

                cb_b = [nextbank(), nextbank()]
                for a in range(2):
                    def cbm(b=cb_b[a], a=a):
                        r = None
                        for l_ in range(32):
                            r = T.matmul(banks[b][0:64, 0:1], lhsT=w1kv[64 * a:64 * a + 64, l_, :], rhs=peT[64 * a:64 * a + 64, l_:l_ + 1], start=(l_ == 0), stop=(l_ == 31))
                        return r
                    k.op(pe, cbm, reads=[W1KV, PET], writes=[BK[cb_b[a]]])
                    k.op(dve, lambda a=a: V.tensor_copy(cbias[:, a:a + 1], banks[cb_b[a]][0:64, 0:1]), reads=[BK[cb_b[a]]], writes=[CBIAS])
                if stop == "C4a1":
                    break
                k.op(dve, lambda: V.memset(kvcR[:, :, 128:130], 0.0), writes=[KVCR])
                k.op(dve, lambda: V.tensor_copy(kvcR[:, :, 0:128], kvcT.rearrange("p (m r) -> p r m", r=16)), reads=KVC + [KVCR], writes=[KVCR])
                hb_ = [nextbank(), nextbank()]
                for a in range(2):
                    def hmm(b=hb_[a], a=a):
                        r = None
                        for l_ in range(32):
                            r = T.matmul(banks[b][0:64, 0:128], lhsT=w1kv[64 * a:64 * a + 64, l_, :],
                                         rhs=kvcR[64 * a:64 * a + 64, l_ % 16, (l_ // 16):(l_ // 16) + 128], start=(l_ == 0), stop=(l_ == 31))
                        return r
                    k.op(pe, hmm, reads=[W1KV, KVCR], writes=[BK[hb_[a]]])
                if stop == "C4a15":
                    break
                for a in range(2):
                    k.op(act, lambda a=a: A_.activation(out=hidkv[:, a, 0:NCMP], in_=banks[hb_[a]][0:64, 0:NCMP], func=AF.Gelu_apprx_tanh, bias=cbias[:, a:a + 1]),
                         reads=[BK[hb_[a]], CBIAS], writes=[HIDKV])
                if stop == "C4a2":
                    break
                b = nextbank()
                k.op(pe, lambda b=b: T.matmul(banks[b][0:NCMP, 0:64], lhsT=hidkv[:, 0, 0:NCMP], rhs=w2kv[:, 0, :], start=True, stop=True), reads=[HIDKV, W2KV], writes=[BK[b]])
                k.op(pe, lambda b=b: T.matmul(banks[b][0:NCMP, 64:128], lhsT=hidkv[:, 1, 0:NCMP], rhs=w2kv[:, 1, :], start=True, stop=True), reads=[HIDKV, W2KV], writes=[BK[b]])
                k.op(act, lambda b=b: A_.copy(vcmp[0:NCMP, 0:64], banks[b][0:NCMP, 64:128]), reads=[BK[b]], writes=[VCMP])
                if stop == "C4a3":
                    break
                k.op(dve, lambda: V.memset(kcr[:], 0.0), writes=[KCR])
                k.op(act, lambda b=b: A_.copy(kcr[0:NCMP, 0:64], banks[b][0:NCMP, 0:64]), reads=[BK[b], KCR], writes=[KCR])
                tq = ftmp[0:NCMP, 0, 0:16].rearrange("p (a d) -> p a d", a=2)
                drain(rope(kcr[0:NCMP, 0:8], kcr[0:NCMP, 8:16], banks[b][0:NCMP, 0:8], banks[b][0:NCMP, 8:16], cosC[0:NCMP, :], sinC[0:NCMP, :],
                           tq[:, 0], tq[:, 1], [BK[b], CSC], [KCR], FTMP))
                k.op(dve, lambda: V.tensor_copy(kcr[0:NCMP, 64:128], kcr[0:NCMP, 0:64]), reads=[KCR], writes=[KCR])
                if stop == "C4a4":
                    break
                b2 = nextbank()
                k.op(pe, lambda b2=b2: T.transpose(bfv(b2)[:, 0:NCMP], kcr[0:NCMP, :], identb[0:NCMP, 0:NCMP]), reads=[KCR, IDB], writes=[BK[b2]])
                k.op(dve, lambda: V.memset(kcmpT[:], 0.0), writes=[KCMPT])
                k.op(dve, lambda b2=b2: V.tensor_copy(kcmpT[0:64, 0, 0:NCMP], bfv(b2)[0:64, 0:NCMP]), reads=[BK[b2], KCMPT], writes=[KCMPT])
                k.op(dve, lambda b2=b2: V.tensor_copy(kcmpT[64:128, 1, 0:NCMP], bfv(b2)[64:128, 0:NCMP]), reads=[BK[b2], KCMPT], writes=[KCMPT])
                if l == 0:
                    tap("kcmpT", kcmpT[:, 0, :], [KCMPT]); tap("vcmp", vcmp[:], [VCMP])
                if stop == "C4a":
                    break

                handoff([KVCR], OACCH2)
                nsa_state = {"acc": 0, "pt": 0}

                def cmp_chain(h, g, own_pt):
                    i0 = 4 * g
                    gcols = slice(g * 512, (g + 1) * 512)
                    oacc = oaccs[g % 2]; OACCH = OACCHS[g % 2]
                    pr, hh = h // 2, h % 2
                    bS = nextbank()
                    k.op(pe, lambda: T.matmul(banks[bS][0:NCMP, 0:512], lhsT=kcmpT[:, hh, 0:NCMP], rhs=qT[:, pr, gcols], start=True, stop=True),
                         reads=[KCMPT] + QT[i0:i0 + 4], writes=[BK[bS]])
                    if own_pt:
                        ptv = PTc[0:NCMP, :]; PTBUF = PTC
                    else:
                        slot = nsa_state["pt"] % 4; nsa_state["pt"] += 1
                        ptv = PTb[0:NCMP, slot, :]; PTBUF = PTB[slot]
                    k.op(act, lambda: A_.activation(out=ptv, in_=banks[bS][0:NCMP, 0:512], func=AF.Exp, scale=SCALE),
                         reads=[BK[bS]], writes=[PTBUF])
                    k.op(pool, lambda: G.affine_select(out=ptv, in_=ptv, pattern=[[1, 512]], compare_op=ALU.is_ge,
                                                       fill=0.0, base=g * 512 - 31, channel_multiplier=-16), reads=[PTBUF], writes=[PTBUF])
                    yield
                    bO = nextbank()
                    def cpv():
                        r = None
                        for ti in range(4):
                            r = T.matmul(banks[bO][:, ti * 97:(ti + 1) * 97], lhsT=ptv[:, ti * 128:(ti + 1) * 128], rhs=vcmp[0:NCMP, :], start=True, stop=True)
                        return r
                    k.op(pe, cpv, reads=[PTBUF, VCMP], writes=[BK[bO]])
                    ov = banks[bO][:, 0:388].rearrange("p (t c) -> p t c", t=4)
                    fi = h % 4
                    rs = fin[:, fi, 0:4]; ri = fin[:, fi, 4:8]; gr = fin[:, fi, 8:12]
                    k.op(dve, lambda: V.tensor_scalar(rs, ov[:, :, 64], 1e-30, None, op0=ALU.max), reads=[BK[bO]], writes=[FIN[fi]])
                    k.op(dve, lambda: V.reciprocal(ri, rs), reads=[FIN[fi]], writes=[FIN[fi]])
                    k.op(dve, lambda: V.tensor_tensor(gr, gsig[:, i0:i0 + 4, 3 * h], ri, op=ALU.mult), reads=[FIN[fi]] + GSIG[i0:i0 + 4], writes=[FIN[fi]])
                    k.op(dve, lambda: V.tensor_tensor(oacc[:, :, h * 64:(h + 1) * 64], ov[:, :, 0:64], gr.unsqueeze(2).to_broadcast([128, 4, 64]), op=ALU.mult),
                         reads=[BK[bO], FIN[fi]], writes=[OACCH[h]])
                    if h == 0:
                        k.op(dve, lambda: V.tensor_tensor(imp[:], ov[:, :, 65:97], ri.unsqueeze(2).to_broadcast([128, 4, 32]), op=ALU.mult),
                             reads=[BK[bO], FIN[fi]], writes=[IMP])
                    else:
                        k.op(dve, lambda: V.tensor_tensor(impm[:], ov[:, :, 65:97], ri.unsqueeze(2).to_broadcast([128, 4, 32]), op=ALU.mult),
                             reads=[BK[bO], FIN[fi]], writes=[IMPM])
                        k.op(dve, lambda: V.tensor_tensor(imp[:], imp[:], impm[:], op=ALU.add), reads=[IMP, IMPM], writes=[IMP])
                    yield

                def selection(g):
                    i0 = 4 * g
                    gcols = slice(g * 512, (g + 1) * 512)
                    k.op(dve, lambda: V.tensor_tensor(impm[:], imp[:], selA[:, i0:i0 + 4, :], op=ALU.mult), reads=[IMP, SELAB], writes=[IMPM])
                    k.op(dve, lambda: V.tensor_tensor(impm[:], impm[:], selB[:, i0:i0 + 4, :], op=ALU.add), reads=[IMPM, SELAB], writes=[IMPM])
                    yield
                    for ti in range(4):
                        k.op(dve, lambda ti=ti: V.max(out=top8[:, ti, :], in_=impm[:, ti, :]), reads=[IMPM], writes=[TOP8S[ti]])
                    yield
                    for ti in range(4):
                        k.op(dve, lambda ti=ti: V.tensor_scalar(selb[:, ti, :], impm[:, ti, :], top8[:, ti, 7:8], NEG, op0=ALU.is_lt, op1=ALU.mult),
                             reads=[IMPM, TOP8S[ti]], writes=[SELBS[ti]])
                    yield
                    b2 = nextbank()
                    def trs(b2=b2):
                        r = None
                        for ti in range(4):
                            r = T.transpose(bfv(b2)[0:32, ti * 128:(ti + 1) * 128], selb[:, ti, :], identb[:])
                        return r
                    k.op(pe, trs, reads=SELBS + [IDB], writes=[BK[b2]])
                    k.op(dve, lambda b2=b2: V.tensor_copy(selbT[0:32, gcols], bfv(b2)[0:32, 0:512]), reads=[BK[b2]], writes=[SELBT[g]])
                    yield

                def pre_gen(g):
                    for h in range(4):
                        yield from cmp_chain(h, g, True)
                    yield from selection(g)

                def att_chain(br, h, g):
                    i0 = 4 * g
                    oacc = oaccs[g % 2]; OACCH = OACCHS[g % 2]
                    pr, hh = h // 2, h % 2
                    rows = slice(64 * hh, 64 * hh + 64)
                    bO = 4 + (nsa_state["acc"] % 4); nsa_state["acc"] += 1
                    k.op(pe, lambda: T.matmul(banks[bO][:, 0:260], lhsT=zerob[:, 0:128], rhs=zerob[:, 0:260], start=True, stop=False),
                         reads=[ZERO], writes=[BK[bO]])
                    yield
                    kt_lo = 0 if br == 0 else max(0, i0 - 4)
                    kk_ = kT if hh == 0 else kThi

                    def qk_list(bS, kt, qlo, qhi, ncol, col0):
                        mms = [(banks[bS][:, 0:ncol], kk_[:, br, kt * 128:(kt + 1) * 128], qT[:, pr, col0:col0 + ncol])]
                        lc = 0
                        if qlo == kt:
                            mms.append((banks[bS][:, 0:128], identb[:], triT[:, 0, :]))
                            lc = 128
                        if br == 0:
                            if ncol > lc:
                                mms.append((banks[bS][:, lc:ncol], ebig[:, kt * 128:(kt + 1) * 128], selbT[:, col0 + lc:col0 + ncol]))
                        else:
                            if qhi == kt + 4:
                                l2 = (qhi - qlo) * 128
                                mms.append((banks[bS][:, l2:l2 + 128], identb[:], triT[:, 1, :]))
                        return mms

                    def emit_qk(mms):
                        r = None
                        for n_, (o_, l_, r_) in enumerate(mms):
                            r = T.matmul(o_, lhsT=l_, rhs=r_, start=(n_ == 0), stop=(n_ == len(mms) - 1))
                        return r

                    def emit_pv(slot, kt, qlo, qhi):
                        r = None
                        for qt in range(qlo, qhi + 1):
                            lc = (qt - qlo) * 128
                            last = (kt == i0 + 3) and (qt == qhi)
                            r = T.matmul(banks[bO][:, (qt - i0) * 65:(qt - i0 + 1) * 65], lhsT=PTb[:, slot, lc:lc + 128], rhs=vaug[:, kt, br, :],
                                         start=False, stop=last)
                        return r

                    steps = []
                    for kt in range(kt_lo, i0 + 4):
                        qlo = max(kt, i0)
                        qhi = i0 + 3 if br == 0 else min(kt + 4, i0 + 3)
                        steps.append((kt, qlo, qhi, (qhi - qlo + 1) * 128, qlo * 128))
                    prev = None
                    for (kt, qlo, qhi, ncol, col0) in steps:
                        bS = nextbank()
                        mms = qk_list(bS, kt, qlo, qhi, ncol, col0)
                        rd = [KT[kt], KTHI, TRI, EBIG, IDB, ZERO, SELBT[g]] + QT[qlo:qhi + 1]
                        wr = [BK[bS]]
                        if prev is not None:
                            pslot, pkt, pqlo, pqhi = prev
                            rd += [PTB[pslot], VAUG[pkt]]
                            wr += [BK[bO]]
                            k.op(pe, lambda prev=prev, mms=mms: (emit_pv(*prev), emit_qk(mms))[1], reads=rd, writes=wr)
                        else:
                            k.op(pe, lambda mms=mms: emit_qk(mms), reads=rd, writes=wr)
                        slot = nsa_state["pt"] % 4; nsa_state["pt"] += 1
                        k.op(act, lambda bS=bS, slot=slot, ncol=ncol: A_.activation(out=PTb[:, slot, 0:ncol], in_=banks[bS][:, 0:ncol], func=AF.Exp, scale=SCALE),
                             reads=[BK[bS]], writes=[PTB[slot]])
                        yield
                        prev = (slot, kt, qlo, qhi)
                    k.op(pe, lambda prev=prev: emit_pv(*prev), reads=[PTB[prev[0]], VAUG[prev[1]]], writes=[BK[bO]])
                    yield
                    ov = banks[bO][:, 0:260].rearrange("p (t c) -> p t c", t=4)
                    fi = bO - 4
                    ri = fin[:, fi, 4:8]; gr = fin[:, fi, 8:12]
                    k.op(dve, lambda: V.reciprocal(ri, ov[:, :, 64]), reads=[BK[bO]], writes=[FIN[fi]])
                    k.op(dve, lambda: V.tensor_tensor(gr, gsig[:, i0:i0 + 4, 3 * h + 1 + br], ri, op=ALU.mult), reads=[FIN[fi]] + GSIG[i0:i0 + 4], writes=[FIN[fi]])
                    k.op(dve, lambda: V.tensor_tensor(ftmp[:], ov[:, :, 0:64], gr.unsqueeze(2).to_broadcast([128, 4, 64]), op=ALU.mult),
                         reads=[BK[bO], FIN[fi]], writes=[FTMP])
                    k.op(dve, lambda: V.tensor_tensor(oacc[:, :, h * 64:(h + 1) * 64], oacc[:, :, h * 64:(h + 1) * 64], ftmp[:], op=ALU.add),
                         reads=[FTMP, OACCH[h]], writes=[OACCH[h]])

                interleave([cmp_chain(h, 0, False) for h in range(4)], 4)
                drain(selection(0))
                ps_state["reserved"] = {4, 5, 6, 7}
                for g in range(4):
                    i0 = 4 * g
                    gcols = slice(g * 512, (g + 1) * 512)
                    oacc = oaccs[g % 2]; OACCH = OACCHS[g % 2]
                    gens = [att_chain(br, h, g) for br in range(2) for h in range(4)]
                    interleave(gens, 4, extra=[pre_gen(g + 1)] if g + 1 < 4 else [])
                    k.op(act, lambda oacc=oacc: A_.copy(nsab[:], oacc[:]), reads=OACCH, writes=[NSAB])
                    b2 = nextbank()
                    def tro(b2=b2):
                        r = None
                        for ti in range(4):
                            for c in range(2):
                                r = T.transpose(bfv(b2)[:, c * 512 + ti * 128:c * 512 + (ti + 1) * 128], nsab[:, ti, c * 128:(c + 1) * 128], identb[:])
                        return r
                    k.op(pe, tro, reads=[NSAB, IDB], writes=[BK[b2]])
                    k.op(dve, lambda b2=b2, gcols=gcols: V.tensor_copy(qT[:, :, gcols], bfv(b2)[:, 0:1024].rearrange("p (c t) -> p c t", c=2)), reads=[BK[b2]], writes=QT[i0:i0 + 4])
                ps_state["reserved"] = set()
                if l == 0:
                    tap("nsaT", qT, QT)
                if stop == "C4":
                    break

                handoff(NSATMP, [WOUT, JUNKD])
                handoff(VTOK + KVC + KT, YD)
                w_out_v = P["w_out"][l].rearrange("(c p) n -> p c n", p=128)
                for c in range(8):
                    if c >= 6:
                        k.dma(wout_c(c), w_out_v[:, c, :], writes=[WOUT], q=pool)
                k.dma(gpost[:], P["post_mix_norm"][l].partition_broadcast(128), writes=[GPOST])
                mixsrc = [(convoT, 0, lambda i: [CONVO[i]]), (convoT, 1, lambda i: [CONVO[i]]),
                          (mixedT, 0, lambda i: [MIXED[0][i // 4]]), (mixedT, 1, lambda i: [MIXED[1][i // 4]]),
                          (uT, 0, lambda i: [UT[0][i // 4]]), (uT, 1, lambda i: [UT[1][i // 4]]),
                          (qT, 0, lambda i: [QT[i]]), (qT, 1, lambda i: [QT[i]])]

                def post_norm_residual(i, yv, YBUFS, junkD=junkD, JUNKD=JUNKD):
                    k.op(act, lambda: A_.activation(out=junkD, in_=yv, func=AF.Square, accum_out=ss[:, i, 0:1]), reads=YBUFS, writes=[JUNKD, SSQ[i]])
                    yield
                    k.op(act, lambda: A_.activation(out=ss[:, i, 1:2], in_=ss[:, i, 0:1], func=AF.Sqrt, scale=1.0 / D, bias=epsb[:, 0:1]), reads=[SSQ[i], EPSB], writes=[SSQ[i]])
                    yield
                    k.op(dve, lambda: V.reciprocal(rstd[:, i:i + 1], ss[:, i, 1:2]), reads=[SSQ[i]], writes=[RSTD[i]])
                    yield
                    k.op(dve, lambda: V.tensor_tensor(yv, yv, gpost[:], op=ALU.mult), reads=YBUFS + [GPOST], writes=YBUFS)
                    yield
                    k.op(dve, lambda: V.scalar_tensor_tensor(x_sb[:, i, :], yv, rstd[:, i:i + 1], x_sb[:, i, :], op0=ALU.mult, op1=ALU.add),
                         reads=YBUFS + [RSTD[i], X[i]], writes=[X[i]])
                    yield

                def d_body(i):
                    ysel = i % 4
                    yv = yD[ysel]
                    YBUFS = [YD[ysel]]
                    for dh in range(2):
                        b = nextbank()
                        def omm(b=b, i=i, dh=dh):
                            r = None
                            for c in range(8):
                                src, cc, _ = mixsrc[c]
                                r = T.matmul(banks[b][:, 0:512], lhsT=src[:, cc, i * 128:(i + 1) * 128], rhs=wout_c(c)[:, dh * 512:(dh + 1) * 512], start=(c == 0), stop=(c == 7))
                            return r
                        rd = [WOUT, WOUTA]
                        for c in range(8):
                            rd += mixsrc[c][2](i)
                        k.op(pe, omm, reads=rd, writes=[BK[b]])
                        yield
                        k.op(act, lambda b=b, yv=yv, dh=dh: A_.copy(yv[:, dh * 512:(dh + 1) * 512], banks[b][:, 0:512]), reads=[BK[b]], writes=YBUFS)
                        yield
                    yield from post_norm_residual(i, yv, YBUFS)
                interleave([d_body(i) for i in range(NT)], 4)
                if l == 0:
                    tap("x1", x_sb[:], X)
                if stop == "D":
                    break

                handoff(CONVO + [WOUT, JUNKD], HT2 + W2S + HNF + [JUNKF])
                handoff(RBMIX, FT + YB)
                handoff([WOUTA], W1S)
                k.dma(gpost[:], P["post_ffn_norm"][l].partition_broadcast(128), writes=[GPOST])
                w1_v = P["ffn_w1"][l].rearrange("(c p) n -> p c n", p=128)
                w2_v = P["ffn_w2"][l].rearrange("(s c p) d -> p s c d", c=4, p=128)
                ps_state["reserved"] = {4, 5, 6, 7}
                def ffn_norm1(tiles):
                    for n_, i in enumerate(tiles):
                        k.op(act, lambda i=i: A_.activation(out=junkF, in_=x_sb[:, i, :], func=AF.Square, accum_out=ss[:, i, 0:1]),
                             reads=[X[i]], writes=[JUNKF, SSQ[i]])
                        k.op(act, lambda i=i: A_.activation(out=ss[:, i, 1:2], in_=ss[:, i, 0:1], func=AF.Sqrt, scale=1.0 / D, bias=epsb[:, 0:1]),
                             reads=[SSQ[i], EPSB], writes=[SSQ[i]])
                        k.op(dve, lambda i=i: V.reciprocal(rstd[:, i:i + 1], ss[:, i, 1:2]), reads=[SSQ[i]], writes=[RSTD[i]])
                        k.op(dve, lambda i=i, n_=n_: V.tensor_scalar(hnF[:, n_, :], x_sb[:, i, :], rstd[:, i:i + 1], None, op0=ALU.mult),
                             reads=[X[i], RSTD[i]], writes=[HNF[n_]])

                def ffn_norm2():
                    hb = [nextbank() for _ in range(4)]
                    for n_ in range(4):
                        def tr8f(n_=n_):
                            r = None
                            for c in range(8):
                                dstp = bfv(hb[c // 2])[:, (c % 2) * 512 + n_ * 128:(c % 2) * 512 + (n_ + 1) * 128]
                                r = T.transpose(dstp, hnF[:, n_, c * 128:(c + 1) * 128], identb[:])
                            return r
                        k.op(pe, tr8f, reads=[HNF[n_], IDB], writes=[BK[b_] for b_ in hb])
                    for c in range(8):
                        b = hb[c // 2]
                        src = bfv(b)[:, (c % 2) * 512:(c % 2) * 512 + 512]
                        if c % 2 == 0:
                            k.op(dve, lambda src=src, c=c: V.tensor_scalar(hT2[:, c, :], src, gcol[:, 1, c:c + 1], None, op0=ALU.mult),
                                 reads=[BK[b], GCOL], writes=[HT2[c]])
                        else:
                            k.op(act, lambda src=src, c=c: A_.activation(out=hT2[:, c, :], in_=src, func=AF.Identity, scale=gcol[:, 1, c:c + 1]),
                                 reads=[BK[b], GCOL], writes=[HT2[c]])

                NW1, NW2 = 3, 3
                ffn_norm1(list(range(0, 4)))
                ffn_norm2()
                for g in range(4):
                    i0 = 4 * g
                    n1 = 16
                    def load_w2(j):
                        dh_, s8 = j // 8, j % 8
                        k.dma(w2s[:, j % NW2], w2_v[:, s8, :, dh_ * 512:(dh_ + 1) * 512], writes=[W2S[j % NW2]], q=pool)
                    def load_w1(s_):
                        k.dma(w1s[:, s_ % NW1], w1_v[:, :, s_ * 256:(s_ + 1) * 256], writes=[W1S[s_ % NW1]], q=pool)
                    if g == 0:
                        for s_ in range(NW1 - 1):
                            load_w1(s_)
                    for j in range(NW2 - 1):
                        load_w2(j)
                    for s_ in range(n1):
                        if s_ + NW1 - 1 < n1:
                            load_w1(s_ + NW1 - 1)
                        bb = [nextbank(), nextbank()]
                        def fmm(bb=bb, s_=s_):
                            r = None
                            for fc in range(2):
                                for c in range(8):
                                    r = T.matmul(banks[bb[fc]][:, 0:512], lhsT=w1s[:, s_ % NW1, c, fc * 128:(fc + 1) * 128], rhs=hT2[:, c, :], start=(c == 0), stop=(c == 7))
                            return r
                        k.op(pe, fmm, reads=HT2 + [W1S[s_ % NW1]], writes=[BK[bb[0]], BK[bb[1]]])
                        for fc in range(2):
                            b = bb[fc]
                            ta = (2 * s_ + fc) % 4
                            rl = tmpA[:, ta, 0:512]
                            k.op(act, lambda b=b, rl=rl: A_.activation(out=rl, in_=banks[b][:, 0:512], func=AF.Relu), reads=[BK[b]], writes=[TMPA[ta]])
                            k.op(dve, lambda rl=rl, s_=s_, fc=fc: V.tensor_tensor(fT[:, 2 * s_ + fc, :], rl, rl, op=ALU.mult), reads=[TMPA[ta]], writes=[FT[2 * s_ + fc]])
                    if g + 1 < 4:
                        ffn_norm1(list(range(i0 + 4, i0 + 8)))
                    for j in range(16):
                        dh, s8 = j // 8, j % 8
                        if j + NW2 - 1 < 16:
                            load_w2(j + NW2 - 1)
                        def wmm(j=j, s8=s8):
                            r = None
                            for ti in range(4):
                                for c in range(4):
                                    r = T.matmul(banks[4 + ti][:, 0:512], lhsT=fT[:, s8 * 4 + c, ti * 128:(ti + 1) * 128], rhs=w2s[:, j % NW2, c, :],
                                                 start=(s8 == 0 and c == 0), stop=(s8 == 7 and c == 3))
                            return r
                        k.op(pe, wmm, reads=FT[s8 * 4:s8 * 4 + 4] + [W2S[j % NW2]], writes=[BK[4], BK[5], BK[6], BK[7]])
                        if s8 == 7:
                            for ti in range(4):
                                k.op(act, lambda ti=ti, dh=dh: A_.copy(ybuf[:, ti, dh * 512:(dh + 1) * 512], banks[4 + ti][:, 0:512]), reads=[BK[4 + ti]], writes=[YB[ti]])
                        if j == 12 and g + 1 < 4:
                            ffn_norm2()
                        if j == 4 and g + 1 < 4:
                            for s_ in range(NW1 - 1):
                                load_w1(s_)
                    interleave([post_norm_residual(i0 + ti, ybuf[:, ti, :], [YB[ti]], junkF, JUNKF) for ti in range(4)], 4)
                ps_state["reserved"] = set()
                if l == 0:
                    tap("x2", x_sb[:], X)

        except _Stop:
            pass
        ov_ = out_d.rearrange("(i p) d -> p i d", p=128)
        OUTB = k.bufs("outd", NT // 2)
        for i in range(0, NT, 2):
            k.dma(ov_[:, i:i + 2, :], x_sb[:, i:i + 2, :], reads=X[i:i + 2], key=OUTB[i // 2])
        for key_, ent in list(k.dma_sems.items()):
            k._wait(k.sp, (ent[0], ent[1]))
    return nc, tap_out


def make_in_maps(inputs, consts):
    maps = []
    shared = {n: np.ascontiguousarray(inputs[n], dtype=np.float32) for n in PARAM_SHAPES}
    for b in range(8):
        m = {"x": np.ascontiguousarray(inputs["x"][b], dtype=np.float32),
             "positions": np.ascontiguousarray(inputs["positions"][b], dtype=np.int32)}
        m.update(shared)
        m.update(consts)
        maps.append(m)
    return maps


_CACHE = {}


def kernel(**inputs):
    inputs = {n: np.asarray(v) for n, v in inputs.items()}
    if "nc" not in _CACHE:
        _CACHE["nc"] = build_program()[0]
    nc = _CACHE["nc"]
    maps = make_in_maps(inputs, host_consts())
    res = run_bass_kernel_spmd(nc, maps, core_ids=list(range(8)))
    return np.stack([np.asarray(r["out"]) for r in res.results], axis=0).astype(np.float32)
```
